# Optimizing a Trainium2 kernel written in Bass

```python
import jax, jax.numpy as jnp
from jax import lax
import numpy as np

D_MODEL = 2048
BATCH = 2
SEQ = 4096
DEPTH = 2

HEAD_DIM = 128
ROPE_THETA = 10000.0
EPS = 1e-6
MASK_VALUE = -1e30
FORCE_VALUE = 1e9
MIN_FORGET = 1e-6
NSA_HEADS = 6
NSA_KV_HEADS = 2
NSA_REP = NSA_HEADS // NSA_KV_HEADS
CMP_BLOCK = 32
CMP_STRIDE = 16
SEL_BLOCK = 64
N_SEL = 16
WINDOW = 512
Q_BLOCK = 128
GDN_HEADS = 5
CONV_WIDTH = 4
GDN_CHUNK = 64
HGRN_HEADS = 5
HGRN_CHUNK = 64
D_FF = 5632

NSA_WIDTH = NSA_HEADS * HEAD_DIM
KV_WIDTH = NSA_KV_HEADS * HEAD_DIM
GDN_WIDTH = GDN_HEADS * HEAD_DIM
HGRN_WIDTH = HGRN_HEADS * HEAD_DIM
MIX_WIDTH = NSA_WIDTH + GDN_WIDTH + HGRN_WIDTH
IN_SIZES = (NSA_WIDTH, KV_WIDTH, KV_WIDTH, KV_WIDTH, KV_WIDTH, KV_WIDTH, KV_WIDTH, 3 * NSA_HEADS,
            GDN_WIDTH, GDN_WIDTH, GDN_WIDTH, GDN_HEADS, GDN_HEADS, GDN_WIDTH,
            HGRN_WIDTH, HGRN_WIDTH, HGRN_WIDTH, HGRN_WIDTH)
IN_WIDTH = NSA_WIDTH + 6 * KV_WIDTH + 3 * NSA_HEADS + 4 * GDN_WIDTH + 2 * GDN_HEADS + 4 * HGRN_WIDTH

kernel_name = 'hymba_nsa_gdn_hgrn2_macaron'


def rms_norm(x, g):
    xf = x.astype(jnp.float32)
    y = xf * lax.rsqrt(jnp.mean(xf * xf, axis=-1, keepdims=True) + EPS)
    return (y * g.astype(jnp.float32)).astype(x.dtype)


def l2norm(x):
    xf = x.astype(jnp.float32)
    return xf * lax.rsqrt(jnp.sum(xf * xf, axis=-1, keepdims=True) + EPS)


def swiglu(x, w_gate, w_up, w_down):
    return (jax.nn.silu(x @ w_gate) * (x @ w_up)) @ w_down


def split_cols(p, sizes):
    offs = np.cumsum(np.array(sizes))[:-1].tolist()
    return jnp.split(p, offs, axis=-1)


def rope_tables(seq):
    inv = 1.0 / (ROPE_THETA ** (jnp.arange(0, HEAD_DIM, 2, dtype=jnp.float32) / HEAD_DIM))
    ang = jnp.arange(seq, dtype=jnp.float32)[:, None] * inv[None, :]
    return jnp.cos(ang), jnp.sin(ang)


def rope(x, cos, sin):
    xf = x.astype(jnp.float32)
    x1, x2 = jnp.split(xf, 2, axis=-1)
    c = cos[None, :, None, :]
    s = sin[None, :, None, :]
    return jnp.concatenate([x1 * c - x2 * s, x2 * c + x1 * s], axis=-1).astype(x.dtype)


def masked_softmax(s, mask):
    s = jnp.where(mask, s.astype(jnp.float32), MASK_VALUE)
    m = jnp.max(s, axis=-1, keepdims=True)
    e = jnp.where(mask, jnp.exp(s - m), 0.0)
    return e / jnp.maximum(jnp.sum(e, axis=-1, keepdims=True), 1e-30)


def masked_decay(diff, causal):
    return jnp.where(causal, jnp.exp(jnp.where(causal, diff, 0.0)), 0.0)


def causal_conv(x, w):
    c = w.shape[1]
    return lax.conv_general_dilated(x, w[:, None, :].astype(x.dtype), window_strides=(1,),
                                    padding=[(CONV_WIDTH - 1, 0)],
                                    dimension_numbers=('NWC', 'WIO', 'NWC'),
                                    feature_group_count=c)


def nsa_mixer(q, k_c, v_c, k_s, v_s, k_w, v_w, gate_logits, pe_k, pe_v, w_k1, w_k2, w_v1, w_v2, cos, sin):
    B, S = q.shape[0], q.shape[1]
    G, R, D = NSA_KV_HEADS, NSA_REP, HEAD_DIM
    scale = HEAD_DIM ** -0.5
    q = (rope(q.reshape(B, S, NSA_HEADS, D), cos, sin) * scale).reshape(B, S, G, R, D)
    k_c = rope(k_c.reshape(B, S, G, D), cos, sin)
    k_s = rope(k_s.reshape(B, S, G, D), cos, sin)
    k_w = rope(k_w.reshape(B, S, G, D), cos, sin)
    v_c = v_c.reshape(B, S, G, D)
    v_s = v_s.reshape(B, S, G, D)
    v_w = v_w.reshape(B, S, G, D)

    n_cmp = (S - CMP_BLOCK) // CMP_STRIDE + 1
    tok = jnp.arange(n_cmp)[:, None] * CMP_STRIDE + jnp.arange(CMP_BLOCK)[None, :]

    def compress(t, pe, w1, w2):
        blocks = t[:, tok] + pe[None, None, :, None, :]
        flat = blocks.transpose(0, 3, 1, 2, 4).reshape(B, G, n_cmp, CMP_BLOCK * D)
        return jax.nn.gelu(flat @ w1) @ w2

    kc = compress(k_c, pe_k, w_k1, w_k2)
    vc = compress(v_c, pe_v, w_v1, w_v2)
    cmp_end = jnp.arange(n_cmp) * CMP_STRIDE + CMP_BLOCK - 1

    n_blk = S // SEL_BLOCK
    n_sel = min(N_SEL, n_blk)
    c_start = jnp.arange(n_cmp) * CMP_STRIDE
    s_start = jnp.arange(n_blk) * SEL_BLOCK
    ov = (jnp.minimum(c_start[:, None] + CMP_BLOCK, s_start[None, :] + SEL_BLOCK)
          - jnp.maximum(c_start[:, None], s_start[None, :]))
    overlap = jnp.clip(ov, 0, None).astype(jnp.float32) / CMP_BLOCK
    ksb = k_s.reshape(B, n_blk, SEL_BLOCK, G, D).transpose(0, 3, 1, 2, 4)
    vsb = v_s.reshape(B, n_blk, SEL_BLOCK, G, D).transpose(0, 3, 1, 2, 4)

    kw = jnp.pad(k_w.transpose(0, 2, 1, 3), ((0, 0), (0, 0), (WINDOW, 0), (0, 0)))
    vw = jnp.pad(v_w.transpose(0, 2, 1, 3), ((0, 0), (0, 0), (WINDOW, 0), (0, 0)))

    n_qb = S // Q_BLOCK
    qb = q.reshape(B, n_qb, Q_BLOCK, G, R, D).transpose(1, 0, 3, 4, 2, 5)
    gb = jax.nn.sigmoid(gate_logits.astype(jnp.float32)).reshape(B, n_qb, Q_BLOCK, G, R, 3)
    gb = gb.transpose(1, 0, 3, 4, 2, 5)
    b_idx = jnp.arange(B)[:, None, None, None]
    g_idx = jnp.arange(G)[None, :, None, None]
    blk = jnp.arange(n_blk)
    sel_off = jnp.arange(SEL_BLOCK)
    win_off = jnp.arange(Q_BLOCK + WINDOW)

    def block(args):
        i, qi, gt = args
        t = i * Q_BLOCK + jnp.arange(Q_BLOCK)
        s_c = jnp.einsum('bgrqd,bgnd->bgrqn', qi, kc).astype(jnp.float32)
        p_c = masked_softmax(s_c, cmp_end[None, :] <= t[:, None])
        o_c = jnp.einsum('bgrqn,bgnd->bgrqd', p_c.astype(vc.dtype), vc)
        imp = jnp.einsum('bgqn,nm->bgqm', jnp.sum(p_c, axis=2), overlap)
        cur = t // SEL_BLOCK
        forced = ((blk[None, :] == 0) | (blk[None, :] == cur[:, None])
                  | (blk[None, :] == cur[:, None] - 1))
        future = blk[None, :] * SEL_BLOCK > t[:, None]
        imp = jnp.where(forced, FORCE_VALUE, jnp.where(future, -FORCE_VALUE, imp))
        _, idx = lax.top_k(imp, n_sel)
        k_sel = ksb[b_idx, g_idx, idx].reshape(B, G, Q_BLOCK, n_sel * SEL_BLOCK, D)
        v_sel = vsb[b_idx, g_idx, idx].reshape(B, G, Q_BLOCK, n_sel * SEL_BLOCK, D)
        kpos = (idx[..., None] * SEL_BLOCK + sel_off).reshape(B, G, 1, Q_BLOCK, n_sel * SEL_BLOCK)
        s_s = jnp.einsum('bgrqd,bgqmd->bgrqm', qi, k_sel).astype(jnp.float32)
        p_s = masked_softmax(s_s, kpos <= t[None, None, None, :, None])
        o_s = jnp.einsum('bgrqm,bgqmd->bgrqd', p_s.astype(v_sel.dtype), v_sel)
        k_win = lax.dynamic_slice_in_dim(kw, i * Q_BLOCK, Q_BLOCK + WINDOW, axis=2)
        v_win = lax.dynamic_slice_in_dim(vw, i * Q_BLOCK, Q_BLOCK + WINDOW, axis=2)
        wpos = i * Q_BLOCK - WINDOW + win_off
        mask_w = ((wpos[None, :] <= t[:, None]) & (wpos[None, :] > t[:, None] - WINDOW)
                  & (wpos[None, :] >= 0))
        s_w = jnp.einsum('bgrqd,bgkd->bgrqk', qi, k_win).astype(jnp.float32)
        p_w = masked_softmax(s_w, mask_w)
        o_w = jnp.einsum('bgrqk,bgkd->bgrqd', p_w.astype(v_win.dtype), v_win)
        o = gt[..., 0:1] * o_c + gt[..., 1:2] * o_s + gt[..., 2:3] * o_w
        return o.astype(qi.dtype)

    o = lax.map(block, (jnp.arange(n_qb), qb, gb))
    return o.transpose(1, 0, 4, 2, 3, 5).reshape(B, S, NSA_WIDTH)


def chunk_gated_delta(q, k, v, g, beta):
    B, S, H, D = q.shape
    C = GDN_CHUNK
    n = S // C

    def chunks(a):
        return a.reshape(B, n, C, H, D).transpose(0, 3, 1, 2, 4)

    q, k, v = chunks(q), chunks(k), chunks(v)
    g = g.reshape(B, n, C, H).transpose(0, 3, 1, 2)
    beta = beta.reshape(B, n, C, H).transpose(0, 3, 1, 2)
    gc = jnp.cumsum(g, axis=-1)
    causal = jnp.tril(jnp.ones((C, C), dtype=bool))
    strict = jnp.tril(jnp.ones((C, C), dtype=bool), -1)
    decay = masked_decay(gc[..., :, None] - gc[..., None, :], causal)
    kb = k * beta[..., None]
    A = jnp.where(strict, jnp.einsum('bhnid,bhnjd->bhnij', kb, k) * decay, 0.0)
    rhs = jnp.concatenate([v * beta[..., None], kb * jnp.exp(gc)[..., None]], axis=-1)
    sol = lax.linalg.triangular_solve(A, rhs, left_side=True, lower=True, unit_diagonal=True)
    u, w = sol[..., :D], sol[..., D:]
    attn = jnp.einsum('bhnid,bhnjd->bhnij', q, k) * decay
    q_dec = q * jnp.exp(gc)[..., None]
    k_dec = k * jnp.exp(gc[..., -1:] - gc)[..., None]
    last = jnp.exp(gc[..., -1])

    def step(state, xs):
        u_, w_, qd, kd, at, ld = xs
        v_new = u_ - jnp.einsum('bhcd,bhde->bhce', w_, state)
        o = jnp.einsum('bhcd,bhde->bhce', qd, state) + jnp.einsum('bhij,bhje->bhie', at, v_new)
        state = state * ld[..., None, None] + jnp.einsum('bhcd,bhce->bhde', kd, v_new)
        return state, o

    xs = tuple(jnp.moveaxis(a, 2, 0) for a in (u, w, q_dec, k_dec, attn, last))
    _, o = lax.scan(step, jnp.zeros((B, H, D, D), jnp.float32), xs)
    return o.transpose(1, 0, 3, 2, 4).reshape(B, S, H, D)


def gated_deltanet(q, k, v, a, b, z, w_conv, a_log, dt_bias, norm_w):
    B, S = q.shape[0], q.shape[1]
    H, D = GDN_HEADS, HEAD_DIM
    qkv = jax.nn.silu(causal_conv(jnp.concatenate([q, k, v], axis=-1), w_conv))
    q, k, v = jnp.split(qkv, 3, axis=-1)
    q = l2norm(q.reshape(B, S, H, D)) * (HEAD_DIM ** -0.5)
    k = l2norm(k.reshape(B, S, H, D))
    v = v.reshape(B, S, H, D).astype(jnp.float32)
    beta = jax.nn.sigmoid(b.astype(jnp.float32))
    g = -jnp.exp(a_log.astype(jnp.float32)) * jax.nn.softplus(a.astype(jnp.float32) + dt_bias.astype(jnp.float32))
    o = chunk_gated_delta(q, k, v, g, beta)
    o = rms_norm(o, norm_w) * jax.nn.silu(z.reshape(B, S, H, D).astype(jnp.float32))
    return o.reshape(B, S, GDN_WIDTH).astype(z.dtype)


def chunk_hgrn2(q, k, v, log_f):
    B, S, H, D = q.shape
    C = HGRN_CHUNK
    n = S // C

    def chunks(a):
        return a.reshape(B, n, C, H, D).transpose(1, 0, 3, 2, 4)

    causal = jnp.tril(jnp.ones((C, C), dtype=bool))[:, :, None]

    def step(state, xs):
        q_, k_, v_, lf = xs
        bcum = jnp.cumsum(lf, axis=-2)
        dec = masked_decay(bcum[..., :, None, :] - bcum[..., None, :, :], causal)
        A = jnp.einsum('bhid,bhjd,bhijd->bhij', q_, k_, dec)
        o = jnp.einsum('bhid,bhde->bhie', q_ * jnp.exp(bcum), state) + jnp.einsum('bhij,bhje->bhie', A, v_)
        b_last = bcum[..., -1:, :]
        state = (state * jnp.exp(bcum[..., -1, :])[..., None]
                 + jnp.einsum('bhjd,bhje->bhde', k_ * jnp.exp(b_last - bcum), v_))
        return state, o

    xs = (chunks(q), chunks(k), chunks(v), chunks(log_f))
    _, o = lax.scan(step, jnp.zeros((B, H, D, D), jnp.float32), xs)
    return o.transpose(1, 0, 3, 2, 4).reshape(B, S, H, D)


def hgrn2(q, f, i, g, lb, norm_w):
    B, S = q.shape[0], q.shape[1]
    H, D = HGRN_HEADS, HEAD_DIM
    f_gate = lb + (1.0 - lb) * jax.nn.sigmoid(f.astype(jnp.float32))
    log_f = jnp.log(jnp.maximum(f_gate, MIN_FORGET))
    k = 1.0 - f_gate
    shp = (B, S, H, D)
    o = chunk_hgrn2(q.astype(jnp.float32).reshape(shp), k.reshape(shp),
                    i.astype(jnp.float32).reshape(shp), log_f.reshape(shp))
    o = rms_norm(o, norm_w) * jax.nn.silu(g.reshape(shp).astype(jnp.float32))
    return o.reshape(B, S, HGRN_WIDTH).astype(g.dtype)


def setup_inputs(seed: int = 0) -> dict:
    key = jax.random.key(seed)
    keys = iter(jax.random.split(key, 48))

    def nrm(shape, scale):
        return jax.random.normal(next(keys), shape, jnp.float32) * scale

    def gain(shape):
        return 1.0 + 0.02 * jax.random.normal(next(keys), shape, jnp.float32)

    dt = jax.random.uniform(next(keys), (DEPTH, GDN_HEADS), jnp.float32, 0.001, 0.1)
    return {
        'x': jax.random.normal(next(keys), (BATCH, SEQ, D_MODEL), jnp.float32),
        'ffn1_norm': gain((DEPTH, D_MODEL)),
        'ffn1_gate': nrm((DEPTH, D_MODEL, D_FF), D_MODEL ** -0.5),
        'ffn1_up': nrm((DEPTH, D_MODEL, D_FF), D_MODEL ** -0.5),
        'ffn1_down': nrm((DEPTH, D_FF, D_MODEL), D_FF ** -0.5),
        'mix_norm': gain((DEPTH, D_MODEL)),
        'w_in': nrm((DEPTH, D_MODEL, IN_WIDTH), D_MODEL ** -0.5),
        'w_out': nrm((DEPTH, MIX_WIDTH, D_MODEL), MIX_WIDTH ** -0.5),
        'nsa_pe_k': nrm((DEPTH, CMP_BLOCK, HEAD_DIM), 0.1),
        'nsa_pe_v': nrm((DEPTH, CMP_BLOCK, HEAD_DIM), 0.1),
        'nsa_ck1': nrm((DEPTH, CMP_BLOCK * HEAD_DIM, HEAD_DIM), (CMP_BLOCK * HEAD_DIM) ** -0.5),
        'nsa_ck2': nrm((DEPTH, HEAD_DIM, HEAD_DIM), HEAD_DIM ** -0.5),
        'nsa_cv1': nrm((DEPTH, CMP_BLOCK * HEAD_DIM, HEAD_DIM), (CMP_BLOCK * HEAD_DIM) ** -0.5),
        'nsa_cv2': nrm((DEPTH, HEAD_DIM, HEAD_DIM), HEAD_DIM ** -0.5),
        'gdn_conv': nrm((DEPTH, CONV_WIDTH, 3 * GDN_WIDTH), CONV_WIDTH ** -0.5),
        'gdn_a_log': jnp.log(jax.random.uniform(next(keys), (DEPTH, GDN_HEADS), jnp.float32, 1.0, 16.0)),
        'gdn_dt_bias': dt + jnp.log(-jnp.expm1(-dt)),
        'gdn_norm': gain((DEPTH, HEAD_DIM)),
        'hgrn_lb': nrm((DEPTH, HGRN_WIDTH), 0.5),
        'hgrn_norm': gain((DEPTH, HEAD_DIM)),
        'ffn2_norm': gain((DEPTH, D_MODEL)),
        'ffn2_gate': nrm((DEPTH, D_MODEL, D_FF), D_MODEL ** -0.5),
        'ffn2_up': nrm((DEPTH, D_MODEL, D_FF), D_MODEL ** -0.5),
        'ffn2_down': nrm((DEPTH, D_FF, D_MODEL), D_FF ** -0.5),
        'final_norm': gain((D_MODEL,)),
    }


def reference(x, ffn1_norm, ffn1_gate, ffn1_up, ffn1_down, mix_norm, w_in, w_out,
              nsa_pe_k, nsa_pe_v, nsa_ck1, nsa_ck2, nsa_cv1, nsa_cv2,
              gdn_conv, gdn_a_log, gdn_dt_bias, gdn_norm, hgrn_lb, hgrn_norm,
              ffn2_norm, ffn2_gate, ffn2_up, ffn2_down, final_norm):
    cos, sin = rope_tables(x.shape[1])
    p_lb = jax.nn.softmax(hgrn_lb.astype(jnp.float32), axis=0)
    lb_all = jnp.cumsum(p_lb, axis=0) - p_lb[0:1]
    for l in range(DEPTH):
        x = x + 0.5 * swiglu(rms_norm(x, ffn1_norm[l]), ffn1_gate[l], ffn1_up[l], ffn1_down[l])
        h = rms_norm(x, mix_norm[l])
        (nq, nkc, nvc, nks, nvs, nkw, nvw, ngt,
         gq, gk, gv, ga, gbeta, gz,
         hq, hf, hi, hg) = split_cols(h @ w_in[l], IN_SIZES)
        y_nsa = nsa_mixer(nq, nkc, nvc, nks, nvs, nkw, nvw, ngt, nsa_pe_k[l], nsa_pe_v[l],
                          nsa_ck1[l], nsa_ck2[l], nsa_cv1[l], nsa_cv2[l], cos, sin)
        y_gdn = gated_deltanet(gq, gk, gv, ga, gbeta, gz, gdn_conv[l], gdn_a_log[l], gdn_dt_bias[l], gdn_norm[l])
        y_hgrn = hgrn2(hq, hf, hi, hg, lb_all[l], hgrn_norm[l])
        y = jnp.concatenate([y_nsa.astype(x.dtype), y_gdn.astype(x.dtype), y_hgrn.astype(x.dtype)], axis=-1)
        x = x + y @ w_out[l]
        x = x + 0.5 * swiglu(rms_norm(x, ffn2_norm[l]), ffn2_gate[l], ffn2_up[l], ffn2_down[l])
    return rms_norm(x, final_norm)
```

```python
import numpy as np
import concourse.bass as bass
import concourse.mybir as mybir
from concourse.bass_utils import run_bass_kernel_spmd

F32 = mybir.dt.float32
BF16 = mybir.dt.bfloat16
AF = mybir.ActivationFunctionType
ALU = mybir.AluOpType
AX = mybir.AxisListType

NCORES = 8
D = 2048
B = 2
S = 4096
NTOK = B * S
TPC = NTOK // NCORES
DFF = 5632
NDC = D // 128
NFC = DFF // 128
EPS = 1e-6
HD = 128
NSA_H, NSA_G, GDN_H, HG_H = 6, 2, 5, 5
IN_SIZES = (768, 256, 256, 256, 256, 256, 256, 18, 640, 640, 640, 5, 5, 640, 640, 640, 640, 640)
IN_OFFS = np.concatenate([[0], np.cumsum(IN_SIZES)]).tolist()
IN_W = IN_OFFS[-1]

ENGS = ['tensor', 'vector', 'scalar', 'gpsimd', 'sync']


class _St:
    def __init__(self, name):
        self.name = name
        self.w = []
        self.r = []
        self.dsem = None
        self.dcnt = 0


class Buf:
    def __init__(self, ap, name=""):
        self.ap = ap
        self.st = _St(name)

    def __getitem__(self, k):
        return self.ap[k]


class Prog:
    def __init__(self, nc):
        self.nc = nc
        self.ops = {e: [] for e in ENGS}
        self.cnt = {e: 0 for e in ENGS}
        self.sem = {e: nc.alloc_semaphore(name="s_" + e) for e in ENGS[:4]}
        self.known = {e: {} for e in ENGS}
        self.semobj = {}
        self.out_tokens = []
        self.nbuf = 0

    def sb(self, shape, dt=F32, name=None):
        self.nbuf += 1
        name = (name or "sb") + f"_{self.nbuf}"
        return Buf(self.nc.alloc_sbuf_tensor(name, list(shape), dt), name)

    def ps(self, shape, dt=F32, name=None):
        self.nbuf += 1
        name = (name or "ps") + f"_{self.nbuf}"
        return Buf(self.nc.alloc_psum_tensor(name, list(shape), dt), name)

    def view(self, ap, name="v"):
        self.nbuf += 1
        return Buf(ap, name + f"_{self.nbuf}")

    def alias_after(self, ap, olds, name="al"):
        b = self.view(ap, name)
        for o in olds:
            b.st.w = b.st.w + o.st.w
            b.st.r = b.st.r + o.st.r
        return b

    def _waits(self, eng, reads, writes):
        toks = []
        for b in reads:
            for t in b.st.w:
                if t[2] == eng and eng == 'tensor':
                    continue
                toks.append(t)
        for b in writes:
            for t in b.st.w + b.st.r:
                if t[2] == eng:
                    continue
                toks.append(t)
        best = {}
        for (k, v, _e) in toks:
            if v > best.get(k, 0):
                best[k] = v
        waits = []
        kn = self.known[eng]
        for k, v in best.items():
            if kn.get(k, 0) >= v:
                continue
            kn[k] = v
            waits.append((k, v))
        return waits

    def op(self, eng, fn, reads=(), writes=()):
        reads = [b for b in reads if isinstance(b, Buf)]
        writes = [b for b in writes if isinstance(b, Buf)]
        waits = self._waits(eng, reads, writes)
        self.cnt[eng] += 1
        key = 'E' + eng
        self.semobj[key] = self.sem[eng]
        tok = (key, self.cnt[eng], eng)
        self.ops[eng].append((waits, fn, (key, 1)))
        for b in reads:
            b.st.r.append(tok)
        for b in writes:
            b.st.w = [tok]
            b.st.r = []
        return tok

    def dma(self, q, out_ap, in_ap, reads=(), writes=(), is_output=False, **kw):
        reads = [b for b in reads if isinstance(b, Buf)]
        writes = [b for b in writes if isinstance(b, Buf)]
        dst = writes[0].st
        waits = self._waits(q, reads, writes)
        if dst.dsem is None:
            dst.dsem = self.nc.alloc_semaphore(name="d_" + dst.name)
        key = 'D' + dst.name
        self.semobj[key] = dst.dsem
        dst.dcnt += 16
        tok = (key, dst.dcnt, None)

        def fn(e, out_ap=out_ap, in_ap=in_ap, kw=kw):
            return e.dma_start(out=out_ap, in_=in_ap, **kw)
        self.ops[q].append((waits, fn, (key, 16)))
        for b in reads:
            b.st.r.append(tok)
        dst.w = [tok]
        dst.r = []
        if is_output:
            self.out_tokens.append(tok)
        return tok

    def finish(self):
        best = {}
        for (k, v, _e) in self.out_tokens:
            best[k] = max(best.get(k, 0), v)
        self.ops['sync'].append((list(best.items()), None, None))
        nc = self.nc
        with nc.Block() as block:
            def mk(engname):
                def body(e):
                    for (waits, fn, inc) in self.ops[engname]:
                        for (k, v) in waits:
                            e.wait_ge(self.semobj[k], v)
                        if fn is None:
                            continue
                        ins = fn(e)
                        if inc is not None:
                            ins.then_inc(self.semobj[inc[0]], inc[1])
                return body
            block.tensor(mk('tensor'))
            block.vector(mk('vector'))
            block.scalar(mk('scalar'))
            block.gpsimd(mk('gpsimd'))
            block.sync(mk('sync'))

    def mm(self, out, lhsT, rhs, start=True, stop=True, reads=(), writes=(), **kw):
        return self.op('tensor', lambda e: e.matmul(out, lhsT, rhs, start=start, stop=stop, **kw),
                       reads, writes)

    def act(self, out, in_, func, reads=(), writes=(), **kw):
        return self.op('scalar', lambda e: e.activation(out, in_, func, **kw), reads, writes)

    def tt(self, eng, out, in0, in1, op, reads=(), writes=()):
        return self.op(eng, lambda e: e.tensor_tensor(out, in0, in1, op=op), reads, writes)

    def ts(self, eng, out, in0, s1, s2, op0, op1=None, reads=(), writes=()):
        if op1 is None:
            return self.op(eng, lambda e: e.tensor_scalar(out, in0, s1, s2, op0=op0), reads, writes)
        return self.op(eng, lambda e: e.tensor_scalar(out, in0, s1, s2, op0=op0, op1=op1), reads, writes)

    def stt(self, eng, out, in0, scalar, in1, op0, op1, reads=(), writes=()):
        return self.op(eng, lambda e: e.scalar_tensor_tensor(out, in0, scalar, in1, op0=op0, op1=op1),
                       reads, writes)

    def copy(self, eng, out, in_, reads=(), writes=()):
        if eng == 'scalar':
            return self.op(eng, lambda e: e.copy(out, in_), reads, writes)
        return self.op(eng, lambda e: e.tensor_copy(out, in_), reads, writes)

    def memset(self, eng, ap, val, writes=()):
        return self.op(eng, lambda e: e.memset(ap, val), (), writes)


def emit_rmsnorm_T(P, xT_dram, g_sb, xn, ntok, ps_pair, ones, scratch):
    xs, sq, rstd = scratch
    nh = ntok // 512
    for dc in range(NDC):
        x_ = xs[dc % 2]
        P.dma('sync', x_[:, 0:ntok], xT_dram[dc * 128:(dc + 1) * 128, :], writes=[x_])
        P.act(sq[:, 0:ntok], x_[:, 0:ntok], AF.Square, reads=[x_], writes=[sq])
        for h in range(nh):
            P.mm(ps_pair[h][:], ones[:], sq[:, h * 512:(h + 1) * 512], start=(dc == 0), stop=(dc == NDC - 1),
                 reads=[ones, sq], writes=[ps_pair[h]])
    for h in range(nh):
        P.act(rstd[:, h * 512:(h + 1) * 512], ps_pair[h][:], AF.Sqrt, reads=[ps_pair[h]], writes=[rstd],
              bias=EPSB[0][:, 0:1], scale=1.0 / D)
    P.op('vector', lambda e: e.reciprocal(rstd[:, 0:ntok], rstd[:, 0:ntok]), [rstd], [rstd])
    for dc in range(NDC):
        x_ = xs[dc % 2]
        P.dma('sync', x_[:, 0:ntok], xT_dram[dc * 128:(dc + 1) * 128, :], writes=[x_])
        P.stt('vector', xn[:, dc, :], x_[:, 0:ntok], g_sb[:, dc:dc + 1], rstd[:, 0:ntok], ALU.mult, ALU.mult,
              reads=[x_, g_sb, rstd], writes=[xn])


EPSB = [None]


def make_consts(P):
    ones = P.sb([128, 128], F32, "ones")
    P.memset('vector', ones[:], 1.0, writes=[ones])
    epsb = P.sb([128, 1], F32, "epsb")
    P.memset('vector', epsb[:], EPS, writes=[epsb])
    EPSB[0] = epsb
    return ones


def build_ffn():
    T = TPC
    nc = bass.Bass("TRN2", target_bir_lowering=False)
    xT = nc.dram_tensor("xT", [D, T], F32, kind="ExternalInput").ap()
    g = nc.dram_tensor("g", [128, NDC], F32, kind="ExternalInput").ap()
    wg = nc.dram_tensor("wg", [D, DFF], F32, kind="ExternalInput").ap()
    wu = nc.dram_tensor("wu", [D, DFF], F32, kind="ExternalInput").ap()
    wd = nc.dram_tensor("wd", [DFF, D], F32, kind="ExternalInput").ap()
    yT = nc.dram_tensor("yT", [D, T], F32, kind="ExternalOutput").ap()
    P = Prog(nc)
    ones = make_consts(P)
    g_sb = P.sb([128, NDC], F32, "g")
    P.dma('sync', g_sb[:], g, writes=[g_sb])
    xn = P.sb([128, NDC, T], BF16, "xn")
    act = P.sb([128, NFC, T], BF16, "act")
    xs = [P.sb([128, T], F32, "xs0"), P.sb([128, T], F32, "xs1")]
    sq = P.sb([128, T], F32, "sq")
    rstd = P.sb([128, T], F32, "rstd")
    sg = [P.sb([128, 512], F32, "sg0"), P.sb([128, 512], F32, "sg1")]
    osb = [P.sb([128, T], F32, "os0"), P.sb([128, T], F32, "os1")]
    banks = [P.ps([128, 512], F32, f"bk{i}") for i in range(8)]
    CB = 256
    NS1 = 3
    warena = nc.alloc_sbuf_tensor("warena", [128, NS1 * 2 * NDC * CB], BF16)
    emit_rmsnorm_T(P, xT, g_sb, xn, T, banks[0:2], ones, (xs, sq, rstd))

    w1 = []
    for s in range(NS1):
        pair = []
        for m in range(2):
            o = (s * 2 + m) * NDC * CB
            pair.append(P.view(warena[:, o:o + NDC * CB].rearrange("p (a c) -> p a c", c=CB), f"w1_{s}_{m}"))
        w1.append(pair)
    wgv = wg.rearrange("(a p) c -> p a c", p=128)
    wuv = wu.rearrange("(a p) c -> p a c", p=128)
    ncb = DFF // CB

    def load1(cb):
        s = cb % NS1
        P.dma('gpsimd', w1[s][0][:], wgv[:, :, cb * CB:(cb + 1) * CB], writes=[w1[s][0]])
        P.dma('gpsimd', w1[s][1][:], wuv[:, :, cb * CB:(cb + 1) * CB], writes=[w1[s][1]])
    load1(0)
    load1(1)
    nh = T // 512
    for cb in range(ncb):
        if cb + 2 < ncb:
            load1(cb + 2)
        s = cb % NS1
        for j in range(CB // 128):
            fc = cb * (CB // 128) + j
            bset = banks[0:4] if fc % 2 == 0 else banks[4:8]
            for m in range(2):
                for h in range(nh):
                    pb = bset[m * 2 + h]
                    for dc in range(NDC):
                        P.mm(pb[:], w1[s][m][:, dc, j * 128:(j + 1) * 128], xn[:, dc, h * 512:(h + 1) * 512],
                             start=(dc == 0), stop=(dc == NDC - 1), reads=[w1[s][m], xn], writes=[pb])
            for h in range(nh):
                sg_ = sg[h % 2]
                P.act(sg_[:], bset[h][:], AF.Silu, reads=[bset[h]], writes=[sg_])
                P.tt('vector', act[:, fc, h * 512:(h + 1) * 512], sg_[:], bset[2 + h][:], ALU.mult,
                     reads=[sg_, bset[2 + h]], writes=[act])

    olds = [w1[s][m] for s in range(NS1) for m in range(2)]
    w2 = []
    for s in range(2):
        o = s * NFC * CB
        w2.append(P.alias_after(warena[:, o:o + NFC * CB].rearrange("p (a c) -> p a c", c=CB), olds, f"w2_{s}"))
    wdv = wd.rearrange("(a p) c -> p a c", p=128)
    ndb = D // CB

    def load2(db):
        P.dma('gpsimd', w2[db % 2][:], wdv[:, :, db * CB:(db + 1) * CB], writes=[w2[db % 2]])
    load2(0)
    outb = P.view(None, "yT_out")
    for db in range(ndb):
        if db + 1 < ndb:
            load2(db + 1)
        wt = w2[db % 2]
        for j in range(CB // 128):
            dc = db * (CB // 128) + j
            x_ = xs[dc % 2]
            P.dma('sync', x_[:], xT[dc * 128:(dc + 1) * 128, :], writes=[x_])
            bset = banks[0:2] if dc % 2 == 0 else banks[2:4]
            o_ = osb[dc % 2]
            for h in range(nh):
                pb = bset[h]
                for fc in range(NFC):
                    P.mm(pb[:], wt[:, fc, j * 128:(j + 1) * 128], act[:, fc, h * 512:(h + 1) * 512],
                         start=(fc == 0), stop=(fc == NFC - 1), reads=[wt, act], writes=[pb])
                P.stt('vector', o_[:, h * 512:(h + 1) * 512], pb[:], 0.5, x_[:, h * 512:(h + 1) * 512],
                      ALU.mult, ALU.add, reads=[pb, x_], writes=[o_])
            P.dma('sync', yT[dc * 128:(dc + 1) * 128, :], o_[:], reads=[o_], writes=[outb], is_output=True)
    P.finish()
    return nc


_CACHE = {}


def _get(name, builder):
    if name not in _CACHE:
        _CACHE[name] = builder()
    return _CACHE[name]


def _launch(nc, in_maps):
    res = run_bass_kernel_spmd(nc, in_maps, core_ids=list(range(NCORES)))
    return res.results


def gain_layout(g):
    return np.ascontiguousarray(np.asarray(g, np.float32).reshape(NDC, 128).T)


def run_ffn(xT_full, gnorm, w_gate, w_up, w_down):
    nc = _get("ffn", build_ffn)
    gl = gain_layout(gnorm)
    wgc = np.ascontiguousarray(w_gate)
    wuc = np.ascontiguousarray(w_up)
    wdc = np.ascontiguousarray(w_down)
    in_maps = []
    for c in range(NCORES):
        in_maps.append({"xT": np.ascontiguousarray(xT_full[:, c * TPC:(c + 1) * TPC]), "g": gl,
                        "wg": wgc, "wu": wuc, "wd": wdc})
    res = _launch(nc, in_maps)
    return np.concatenate([r["yT"] for r in res], axis=1)


INP = 7680


def build_inproj():
    T = TPC
    nc = bass.Bass("TRN2", target_bir_lowering=False)
    xT = nc.dram_tensor("xT", [D, T], F32, kind="ExternalInput").ap()
    g = nc.dram_tensor("g", [128, NDC], F32, kind="ExternalInput").ap()
    w = nc.dram_tensor("w", [D, INP], F32, kind="ExternalInput").ap()
    pT = nc.dram_tensor("pT", [INP, T], F32, kind="ExternalOutput").ap()
    P = Prog(nc)
    ones = make_consts(P)
    g_sb = P.sb([128, NDC], F32, "g")
    P.dma('sync', g_sb[:], g, writes=[g_sb])
    xn = P.sb([128, NDC, T], BF16, "xn")
    xs = [P.sb([128, T], F32, "xs0"), P.sb([128, T], F32, "xs1")]
    sq = P.sb([128, T], F32, "sq")
    rstd = P.sb([128, T], F32, "rstd")
    osb = [P.sb([128, T], F32, f"os{i}") for i in range(4)]
    banks = [P.ps([128, 512], F32, f"bk{i}") for i in range(8)]
    CB = 256
    NS = 3
    ws = [P.sb([128, NDC, CB], BF16, f"w{s}") for s in range(NS)]
    emit_rmsnorm_T(P, xT, g_sb, xn, T, banks[0:2], ones, (xs, sq, rstd))
    wv = w.rearrange("(a p) c -> p a c", p=128)
    ncb = INP // CB

    def load(cb):
        P.dma('gpsimd', ws[cb % NS][:], wv[:, :, cb * CB:(cb + 1) * CB], writes=[ws[cb % NS]])
    load(0)
    load(1)
    nh = T // 512
    outb = P.view(None, "pT_out")
    k = 0
    for cb in range(ncb):
        if cb + 2 < ncb:
            load(cb + 2)
        wt = ws[cb % NS]
        for j in range(CB // 128):
            oc = cb * (CB // 128) + j
            o_ = osb[oc % 4]
            for h in range(nh):
                pb = banks[k % 8]
                k += 1
                for dc in range(NDC):
                    P.mm(pb[:], wt[:, dc, j * 128:(j + 1) * 128], xn[:, dc, h * 512:(h + 1) * 512],
                         start=(dc == 0), stop=(dc == NDC - 1), reads=[wt, xn], writes=[pb])
                if h % 2 == 0:
                    P.copy('vector', o_[:, h * 512:(h + 1) * 512], pb[:], reads=[pb], writes=[o_])
                else:
                    P.copy('scalar', o_[:, h * 512:(h + 1) * 512], pb[:], reads=[pb], writes=[o_])
            P.dma('sync', pT[oc * 128:(oc + 1) * 128, :], o_[:], reads=[o_], writes=[outb], is_output=True)
    P.finish()
    return nc


def run_inproj(xT_full, gnorm, w_in):
    nc = _get("inproj", build_inproj)
    gl = gain_layout(gnorm)
    wp = np.zeros((D, INP), np.float32)
    wp[:, :IN_W] = w_in
    in_maps = [{"xT": np.ascontiguousarray(xT_full[:, c * TPC:(c + 1) * TPC]), "g": gl, "w": wp}
               for c in range(NCORES)]
    res = _launch(nc, in_maps)
    return np.concatenate([r["pT"] for r in res], axis=1)


def build_outproj():
    T = TPC
    nc = bass.Bass("TRN2", target_bir_lowering=False)
    xT = nc.dram_tensor("xT", [D, T], F32, kind="ExternalInput").ap()
    yT = nc.dram_tensor("yT", [D, T], F32, kind="ExternalInput").ap()
    w = nc.dram_tensor("w", [D, D], F32, kind="ExternalInput").ap()
    oT = nc.dram_tensor("oT", [D, T], F32, kind="ExternalOutput").ap()
    P = Prog(nc)
    yb = P.sb([128, NDC, T], BF16, "yb")
    P.dma('gpsimd', yb[:], yT.rearrange("(a p) t -> p a t", p=128), writes=[yb])
    xs = [P.sb([128, T], F32, "xs0"), P.sb([128, T], F32, "xs1")]
    osb = [P.sb([128, T], F32, f"os{i}") for i in range(2)]
    banks = [P.ps([128, 512], F32, f"bk{i}") for i in range(8)]
    CB = 256
    ws = [P.sb([128, NDC, CB], BF16, f"w{s}") for s in range(2)]
    wv = w.rearrange("(a p) c -> p a c", p=128)
    ncb = D // CB

    def load(cb):
        P.dma('gpsimd', ws[cb % 2][:], wv[:, :, cb * CB:(cb + 1) * CB], writes=[ws[cb % 2]])
    load(0)
    nh = T // 512
    outb = P.view(None, "oT_out")
    k = 0
    for cb in range(ncb):
        if cb + 1 < ncb:
            load(cb + 1)
        wt = ws[cb % 2]
        for j in range(CB // 128):
            oc = cb * (CB // 128) + j
            x_ = xs[oc % 2]
            P.dma('sync', x_[:], xT[oc * 128:(oc + 1) * 128, :], writes=[x_])
            o_ = osb[oc % 2]
            for h in range(nh):
                pb = banks[k % 8]
                k += 1
                for dc in range(NDC):
                    P.mm(pb[:], wt[:, dc, j * 128:(j + 1) * 128], yb[:, dc, h * 512:(h + 1) * 512],
                         start=(dc == 0), stop=(dc == NDC - 1), reads=[wt, yb], writes=[pb])
                P.tt('vector', o_[:, h * 512:(h + 1) * 512], pb[:], x_[:, h * 512:(h + 1) * 512], ALU.add,
                     reads=[pb, x_], writes=[o_])
            P.dma('sync', oT[oc * 128:(oc + 1) * 128, :], o_[:], reads=[o_], writes=[outb], is_output=True)
    P.finish()
    return nc


def run_outproj(xT_full, yT_full, w_out):
    nc = _get("outproj", build_outproj)
    wc = np.ascontiguousarray(w_out)
    in_maps = [{"xT": np.ascontiguousarray(xT_full[:, c * TPC:(c + 1) * TPC]),
                "yT": np.ascontiguousarray(yT_full[:, c * TPC:(c + 1) * TPC]), "w": wc}
               for c in range(NCORES)]
    res = _launch(nc, in_maps)
    return np.concatenate([r["oT"] for r in res], axis=1)


CH = 64
NCH = S // CH
NSLOT = 2


def make_ident(P, n=128, dt=F32):
    ident = P.sb([n, n], dt, "ident")
    P.memset('gpsimd', ident[:], 1.0, writes=[ident])
    P.op('gpsimd', lambda e: e.affine_select(ident[:], ident[:], pattern=[[-1, n]], compare_op=ALU.is_equal,
                                             fill=0.0, base=0, channel_multiplier=1), [ident], [ident])
    return ident


def make_triu(P, n=64, dt=F32, name="triu"):
    m = P.sb([n, n], dt, name)
    P.memset('gpsimd', m[:], 1.0, writes=[m])
    P.op('gpsimd', lambda e: e.affine_select(m[:], m[:], pattern=[[1, n]], compare_op=ALU.is_ge,
                                             fill=0.0, base=0, channel_multiplier=-1), [m], [m])
    return m


def build_hgrn():
    nc = bass.Bass("TRN2", target_bir_lowering=False)
    ins = []
    for s in range(NSLOT):
        d = {}
        for nm in ("hq", "hf", "hg"):
            d[nm] = nc.dram_tensor(f"{nm}{s}", [128, S], F32, kind="ExternalInput").ap()
        d["hv"] = nc.dram_tensor(f"hv{s}", [S, 128], F32, kind="ExternalInput").ap()
        d["par"] = nc.dram_tensor(f"par{s}", [128, 4], F32, kind="ExternalInput").ap()
        d["out"] = nc.dram_tensor(f"hy{s}", [128, S], F32, kind="ExternalOutput").ap()
        ins.append(d)
    P = Prog(nc)
    ones = make_consts(P)
    ident = make_ident(P)
    triu = make_triu(P, 64, F32)
    A = P.sb([128, S], F32, "A")
    Bt = P.sb([128, S], F32, "Bt")
    C = P.sb([128, S], F32, "C")
    Ft = P.sb([128, S], F32, "Ft")
    Gt = P.sb([128, S], F32, "Gt")
    Ht = P.sb([128, S], F32, "Ht")
    onesb = P.sb([128, S], BF16, "onesb")
    P.memset('gpsimd', onesb[:], 1.0, writes=[onesb])
    qs = P.sb([128, S], BF16, "qs")
    qm = P.sb([128, S], BF16, "qm")
    km = P.sb([128, S], BF16, "km")
    vt = P.sb([CH, NCH, 128], BF16, "vt")
    kst = P.sb([CH, NCH, 128], BF16, "kst")
    par = P.sb([128, 8], F32, "par")
    ebl = P.sb([128, NCH], F32, "ebl")
    bmid = P.sb([128, NCH], F32, "bmid")
    Sst = P.sb([128, 128], F32, "Sst")
    Sbf = P.sb([128, 128], BF16, "Sbf")
    At = [P.sb([CH, CH], BF16, f"At{i}") for i in range(2)]
    banks = [P.ps([128, 512], F32, f"bk{i}") for i in range(8)]

    def v3(t):
        return t[:].rearrange("p (c j) -> p c j", j=CH)

    for s in range(NSLOT):
        d = ins[s]
        outb = P.view(None, f"hy{s}")
        P.dma('sync', par[:, 0:4], d["par"], writes=[par])
        P.dma('sync', Gt[:], d["hq"], writes=[Gt])
        P.dma('sync', A[:], d["hf"], writes=[A])
        P.dma('gpsimd', vt[:], d["hv"].rearrange("(c j) e -> j c e", j=CH), writes=[vt])
        P.tt('vector', par[:, 4:5], par[:, 1:2], par[:, 0:1], ALU.subtract, reads=[par], writes=[par])
        P.act(par[:, 4:5], par[:, 4:5], AF.Sigmoid, reads=[par], writes=[par])
        P.tt('vector', par[:, 4:5], par[:, 4:5], par[:, 2:3], ALU.mult, reads=[par], writes=[par])
        P.ts('vector', par[:, 5:6], par[:, 4:5], -1.0, 1.0, ALU.mult, ALU.add, reads=[par], writes=[par])
        P.act(A[:], A[:], AF.Sigmoid, reads=[A], writes=[A])
        P.ts('vector', A[:], A[:], par[:, 5:6], par[:, 4:5], ALU.mult, ALU.add, reads=[A, par], writes=[A])
        P.ts('gpsimd', C[:], A[:], -1.0, 1.0, ALU.mult, ALU.add, reads=[A], writes=[C])
        P.ts('vector', Bt[:], A[:], 1e-6, None, ALU.max, reads=[A], writes=[Bt])
        P.act(Bt[:], Bt[:], AF.Ln, reads=[Bt], writes=[Bt])
        P.op('vector', lambda e: e.tensor_tensor_scan(Ft[:], onesb[:], Bt[:], 0.0, op0=ALU.mult, op1=ALU.add),
             [onesb, Bt], [Ft])
        bg3 = v3(Ft)
        b3 = v3(A)
        P.copy('vector', b3[:, 0, :], bg3[:, 0, :], reads=[Ft], writes=[A])
        P.tt('vector', b3[:, 1:, :], bg3[:, 1:, :], bg3[:, 0:NCH - 1, CH - 1:CH].to_broadcast([128, NCH - 1, CH]),
             ALU.subtract, reads=[Ft], writes=[A])
        P.act(Bt[:], A[:], AF.Exp, reads=[A], writes=[Bt])
        P.tt('vector', qs[:], Gt[:], Bt[:], ALU.mult, reads=[Gt, Bt], writes=[qs])
        P.copy('vector', ebl[:], v3(Bt)[:, :, CH - 1], reads=[Bt], writes=[ebl])
        P.tt('vector', Ft[:].rearrange("p (c j) -> p c j", j=CH), b3,
             b3[:, :, CH - 1:CH].to_broadcast([128, NCH, CH]), ALU.subtract, reads=[A], writes=[Ft])
        P.act(Ft[:], Ft[:], AF.Exp, reads=[Ft], writes=[Ft], scale=-1.0)
        P.tt('vector', Ft[:], Ft[:], C[:], ALU.mult, reads=[Ft, C], writes=[Ft])
        P.tt('vector', v3(Bt), b3, b3[:, :, CH // 2 - 1:CH // 2].to_broadcast([128, NCH, CH]), ALU.subtract,
             reads=[A], writes=[Bt])
        P.act(Ht[:], Bt[:], AF.Exp, reads=[Bt], writes=[Ht])
        P.tt('vector', qm[:], Gt[:], Ht[:], ALU.mult, reads=[Gt, Ht], writes=[qm])
        P.act(Ht[:], Bt[:], AF.Exp, reads=[Bt], writes=[Ht], scale=-1.0)
        P.tt('vector', km[:], C[:], Ht[:], ALU.mult, reads=[C, Ht], writes=[km])
        for c4 in range(NCH // 4):
            pb = banks[6 + c4 % 2]
            for i in range(4):
                c = c4 * 4 + i
                P.op('tensor', lambda e, pb=pb, i=i, c=c: e.transpose(pb[0:CH, i * 128:(i + 1) * 128],
                                                                       Ft[:, c * CH:(c + 1) * CH], ident[:]),
                     [Ft, ident], [pb])
            P.copy('scalar' if c4 % 2 else 'vector', kst[:, c4 * 4:(c4 + 1) * 4, :],
                   pb[0:CH, :].rearrange("p (a d) -> p a d", d=128), reads=[pb], writes=[kst])
        P.dma('sync', C[:], d["hg"], writes=[C])
        P.memset('vector', Sst[:], 0.0, writes=[Sst])
        P.memset('gpsimd', Sbf[:], 0.0, writes=[Sbf])
        for c in range(NCH):
            cs = slice(c * CH, (c + 1) * CH)
            pA = banks[c % 2]
            pO = banks[2 + c % 2]
            pS = banks[4 + c % 2]
            at = At[c % 2]
            P.mm(pA[0:CH, 0:CH], km[:, cs], qm[:, cs], reads=[km, qm], writes=[pA])
            P.tt('vector', at[:], pA[0:CH, 0:CH], triu[:], ALU.mult, reads=[pA, triu], writes=[at])
            P.mm(pO[:, 0:CH], Sbf[:], qs[:, cs], start=True, stop=False, reads=[Sbf, qs], writes=[pO])
            P.mm(pO[:, 0:CH], vt[:, c, :], at[:], start=False, stop=True, reads=[vt, at], writes=[pO])
            P.copy('scalar', Ht[:, cs], pO[:, 0:CH], reads=[pO], writes=[Ht])
            P.mm(pS[:, 0:128], kst[:, c, :], vt[:, c, :], reads=[kst, vt], writes=[pS])
            P.stt('vector', Sst[:], Sst[:], ebl[:, c:c + 1], pS[:, 0:128], ALU.mult, ALU.add,
                  reads=[Sst, ebl, pS], writes=[Sst])
            P.copy('scalar', Sbf[:], Sst[:], reads=[Sst], writes=[Sbf])
        P.act(A[:], Ht[:], AF.Square, reads=[Ht], writes=[A])
        for h in range(S // 512):
            pb = banks[h % 8]
            P.mm(pb[:], ones[:], A[:, h * 512:(h + 1) * 512], reads=[ones, A], writes=[pb])
            P.act(Bt[:, h * 512:(h + 1) * 512], pb[:], AF.Sqrt, reads=[pb], writes=[Bt],
                  bias=EPSB[0][:, 0:1], scale=1.0 / 128)
        P.op('vector', lambda e: e.reciprocal(Bt[:], Bt[:]), [Bt], [Bt])
        P.stt('vector', Ht[:], Ht[:], par[:, 3:4], Bt[:], ALU.mult, ALU.mult, reads=[Ht, par, Bt], writes=[Ht])
        P.act(C[:], C[:], AF.Silu, reads=[C], writes=[C])
        P.tt('vector', Ht[:], Ht[:], C[:], ALU.mult, reads=[Ht, C], writes=[Ht])
        P.dma('sync', d["out"], Ht[:], reads=[Ht], writes=[outb], is_output=True)
    P.finish()
    return nc


def run_hgrn(pT, layer, hgrn_lb, hgrn_norm):
    nc = _get("hgrn", build_hgrn)
    o_q, o_f, o_i, o_g = IN_OFFS[14], IN_OFFS[15], IN_OFFS[16], IN_OFFS[17]
    units = [(b, h) for b in range(B) for h in range(HG_H)]
    in_maps = []
    slots = []
    for c in range(NCORES):
        m = {}
        us = []
        for s in range(NSLOT):
            ui = c + s * NCORES
            if ui >= len(units):
                ui = c
            b, h = units[ui]
            us.append((b, h, ui == c + s * NCORES))
            ts = slice(b * S, (b + 1) * S)
            rs = slice(h * 128, (h + 1) * 128)
            m[f"hq{s}"] = np.ascontiguousarray(pT[o_q + h * 128:o_q + (h + 1) * 128, ts])
            m[f"hf{s}"] = np.ascontiguousarray(pT[o_f + h * 128:o_f + (h + 1) * 128, ts])
            m[f"hg{s}"] = np.ascontiguousarray(pT[o_g + h * 128:o_g + (h + 1) * 128, ts])
            m[f"hv{s}"] = np.ascontiguousarray(pT[o_i + h * 128:o_i + (h + 1) * 128, ts].T)
            par = np.zeros((128, 4), np.float32)
            par[:, 0] = hgrn_lb[0, rs]
            par[:, 1] = hgrn_lb[1, rs]
            par[:, 2] = 1.0 if layer == 1 else 0.0
            par[:, 3] = hgrn_norm[layer]
            m[f"par{s}"] = par
        in_maps.append(m)
        slots.append(us)
    res = _launch(nc, in_maps)
    y = np.zeros((HG_H * 128, NTOK), np.float32)
    for c in range(NCORES):
        for s in range(NSLOT):
            b, h, real = slots[c][s]
            if real:
                y[h * 128:(h + 1) * 128, b * S:(b + 1) * S] = res[c][f"hy{s}"]
    return y


def mask8(P, kind, name):
    m = P.sb([CH, 8 * CH], F32, name)
    P.memset('gpsimd', m[:], 1.0, writes=[m])
    for g8 in range(8):
        sl = m[:, g8 * CH:(g8 + 1) * CH]
        if kind == 'eye':
            kw = dict(pattern=[[-1, CH]], compare_op=ALU.is_equal, base=0, channel_multiplier=1)
        elif kind == 'ugt':
            kw = dict(pattern=[[1, CH]], compare_op=ALU.is_gt, base=0, channel_multiplier=-1)
        elif kind == 'uge':
            kw = dict(pattern=[[1, CH]], compare_op=ALU.is_ge, base=0, channel_multiplier=-1)
        else:
            kw = dict(pattern=[[-1, CH]], compare_op=ALU.is_gt, base=0, channel_multiplier=1)
        P.op('gpsimd', lambda e, sl=sl, kw=kw: e.affine_select(sl, sl, fill=0.0, **kw), [m], [m])
    return m


def build_gdn():
    nc = bass.Bass("TRN2", target_bir_lowering=False)
    ins = []
    for s in range(NSLOT):
        d = {}
        for nm in ("gq", "gk", "gv", "gab", "gbb"):
            d[nm] = nc.dram_tensor(f"{nm}{s}", [128, S], F32, kind="ExternalInput").ap()
        d["gz"] = nc.dram_tensor(f"gz{s}", [S, 128], F32, kind="ExternalInput").ap()
        d["par"] = nc.dram_tensor(f"gpar{s}", [128, 16], F32, kind="ExternalInput").ap()
        d["nwb"] = nc.dram_tensor(f"gnw{s}", [CH, 8 * 128], F32, kind="ExternalInput").ap()
        d["out"] = nc.dram_tensor(f"gy{s}", [S, 128], F32, kind="ExternalOutput").ap()
        ins.append(d)
    P = Prog(nc)
    ones = make_consts(P)
    ident = make_ident(P)
    I8 = mask8(P, 'eye', "I8")
    UGT8 = mask8(P, 'ugt', "UGT8")
    LGT8 = mask8(P, 'lgt', "LGT8")
    farena = nc.alloc_sbuf_tensor("farena", [128, 6 * S], F32)
    F = [P.view(farena[:, i * S:(i + 1) * S], f"F{i}") for i in range(6)]
    q_b = P.sb([128, S], BF16, "q_b")
    qd_b = P.sb([128, S], BF16, "qd_b")
    k_b = P.sb([128, S], BF16, "k_b")
    kb_b = P.sb([128, S], BF16, "kb_b")
    attT = P.sb([CH, S], BF16, "attT")
    kbe_t = P.sb([CH, NCH, 128], BF16, "kbe_t")
    kdec_t = P.sb([CH, NCH, 128], BF16, "kdec_t")
    vb_t = P.sb([CH, NCH, 128], BF16, "vb_t")
    par = P.sb([128, 16], F32, "par")
    nwb = P.sb([CH, 8 * 128], F32, "nwb")
    egl = P.sb([128, NCH], F32, "egl")
    gct = P.sb([CH, NCH], F32, "gct")
    ss_t = P.sb([CH, 16], F32, "ss_t")
    Sst = P.sb([128, 128], F32, "Sst")
    Sbf = P.sb([128, 128], BF16, "Sbf")
    vn = [P.sb([CH, 128], BF16, f"vn{i}") for i in range(2)]
    banks = [P.ps([128, 512], F32, f"bk{i}") for i in range(8)]
    SCALE = float(HD) ** -0.5
    GS = 4

    def v3(ap):
        return ap.rearrange("p (c j) -> p c j", j=CH)

    def transposes(src, dst):
        for c4 in range(NCH // 4):
            pb = banks[6 + c4 % 2]
            for i in range(4):
                c = c4 * 4 + i
                P.op('tensor', lambda e, pb=pb, i=i, c=c: e.transpose(pb[0:CH, i * 128:(i + 1) * 128],
                                                                       src[:, c * CH:(c + 1) * CH], ident[:]),
                     [src, ident], [pb])
            P.copy('scalar' if c4 % 2 else 'vector', dst[:, c4 * 4:(c4 + 1) * 4, :],
                   pb[0:CH, :].rearrange("p (a d) -> p a d", d=128), reads=[pb], writes=[dst])

    def conv_silu(dram, dst, wcol):
        X = F[0]
        P.dma('sync', X[:], dram, writes=[X])
        P.ts('vector', dst[:], X[:], par[:, wcol + 3:wcol + 4], None, ALU.mult, reads=[X, par], writes=[dst])
        for sh in (1, 2, 3):
            P.stt('vector', dst[:, sh:S], X[:, 0:S - sh], par[:, wcol + 3 - sh:wcol + 4 - sh], dst[:, sh:S],
                  ALU.mult, ALU.add, reads=[X, par, dst], writes=[dst])
        P.act(dst[:], dst[:], AF.Silu, reads=[dst], writes=[dst])

    def l2norm(t, tmp, rs, scale):
        P.act(tmp[:], t[:], AF.Square, reads=[t], writes=[tmp])
        for h in range(S // 512):
            pb = banks[h % 8]
            P.mm(pb[:], ones[:], tmp[:, h * 512:(h + 1) * 512], reads=[ones, tmp], writes=[pb])
            P.act(rs[:, h * 512:(h + 1) * 512], pb[:], AF.Sqrt, reads=[pb], writes=[rs], bias=EPSB[0][:, 0:1], scale=1.0)
        P.op('vector', lambda e: e.reciprocal(rs[:], rs[:]), [rs], [rs])
        P.stt('vector', t[:], t[:], scale, rs[:], ALU.mult, ALU.mult, reads=[t, rs], writes=[t])

    for s in range(NSLOT):
        d = ins[s]
        outb = P.view(None, f"gy{s}")
        P.dma('sync', par[:], d["par"], writes=[par])
        P.dma('sync', nwb[:], d["nwb"], writes=[nwb])
        conv_silu(d["gq"], F[1], 0)
        conv_silu(d["gk"], F[2], 4)
        conv_silu(d["gv"], F[3], 8)
        l2norm(F[1], F[0], F[4], SCALE)
        l2norm(F[2], F[0], F[4], 1.0)
        P.copy('gpsimd', q_b[:], F[1][:], reads=[F[1]], writes=[q_b])
        P.copy('gpsimd', k_b[:], F[2][:], reads=[F[2]], writes=[k_b])
        P.dma('sync', F[4][:], d["gbb"], writes=[F[4]])
        P.act(F[4][:], F[4][:], AF.Sigmoid, reads=[F[4]], writes=[F[4]])
        P.tt('vector', F[3][:], F[3][:], F[4][:], ALU.mult, reads=[F[3], F[4]], writes=[F[3]])
        transposes(F[3], vb_t)
        P.tt('gpsimd', F[5][:], F[2][:], F[4][:], ALU.mult, reads=[F[2], F[4]], writes=[F[5]])
        P.copy('gpsimd', kb_b[:], F[5][:], reads=[F[5]], writes=[kb_b])
        P.dma('sync', F[0][:], d["gab"], writes=[F[0]])
        P.act(par[:, 14:15], par[:, 12:13], AF.Exp, reads=[par], writes=[par])
        P.ts('vector', par[:, 15:16], par[:, 14:15], -1.0, None, ALU.mult, reads=[par], writes=[par])
        P.ts('vector', F[0][:], F[0][:], par[:, 13:14], None, ALU.add, reads=[F[0], par], writes=[F[0]])
        P.act(F[4][:], F[0][:], AF.Abs, reads=[F[0]], writes=[F[4]])
        P.act(F[4][:], F[4][:], AF.Exp, reads=[F[4]], writes=[F[4]], scale=-1.0)
        P.act(F[4][:], F[4][:], AF.Ln, reads=[F[4]], writes=[F[4]], bias=1.0)
        P.ts('vector', F[0][:], F[0][:], 0.0, None, ALU.max, reads=[F[0]], writes=[F[0]])
        P.tt('vector', F[0][:], F[0][:], F[4][:], ALU.add, reads=[F[0], F[4]], writes=[F[0]])
        P.ts('vector', F[0][:], F[0][:], par[:, 15:16], None, ALU.mult, reads=[F[0], par], writes=[F[0]])
        P.op('vector', lambda e: e.tensor_tensor_scan(F[4][:], F[0][:], F[0][:], 0.0, op0=ALU.add, op1=ALU.min),
             [F[0]], [F[4]])
        g3 = v3(F[4][:])
        c3 = v3(F[0][:])
        P.copy('vector', c3[:, 0, :], g3[:, 0, :], reads=[F[4]], writes=[F[0]])
        P.tt('vector', c3[:, 1:, :], g3[:, 1:, :], g3[:, 0:NCH - 1, CH - 1:CH].to_broadcast([128, NCH - 1, CH]),
             ALU.subtract, reads=[F[4]], writes=[F[0]])
        P.act(F[4][:], F[0][:], AF.Exp, reads=[F[0]], writes=[F[4]])
        P.copy('vector', egl[:], v3(F[4][:])[:, :, CH - 1], reads=[F[4]], writes=[egl])
        P.tt('vector', F[5][:], F[5][:], F[4][:], ALU.mult, reads=[F[5], F[4]], writes=[F[5]])
        transposes(F[5], kbe_t)
        P.tt('vector', qd_b[:], F[1][:], F[4][:], ALU.mult, reads=[F[1], F[4]], writes=[qd_b])
        P.tt('vector', v3(F[4][:]), c3, c3[:, :, CH - 1:CH].to_broadcast([128, NCH, CH]), ALU.subtract,
             reads=[F[0]], writes=[F[4]])
        P.act(F[4][:], F[4][:], AF.Exp, reads=[F[4]], writes=[F[4]], scale=-1.0)
        P.tt('vector', F[2][:], F[2][:], F[4][:], ALU.mult, reads=[F[2], F[4]], writes=[F[2]])
        transposes(F[2], kdec_t)
        Ds = P.alias_after(farena[0:CH, 4 * S:5 * S], [F[4]], "Ds")
        DTs = P.alias_after(farena[0:CH, 5 * S:6 * S], [F[5]], "DTs")
        gr = v3(F[0][0:CH, :])
        P.tt('vector', v3(Ds[:]), gr, I8[:, 0:CH].unsqueeze(1).to_broadcast([CH, NCH, CH]), ALU.mult,
             reads=[F[0], I8], writes=[Ds])
        P.op('vector', lambda e: e.tensor_reduce(gct[:], v3(Ds[:]), axis=AX.X, op=ALU.add), [Ds], [gct])
        P.tt('vector', v3(DTs[:]), gr, gct[:].unsqueeze(2).to_broadcast([CH, NCH, CH]), ALU.subtract,
             reads=[F[0], gct], writes=[DTs])
        P.ts('gpsimd', Ds[:], DTs[:], 0.0, None, ALU.max, reads=[DTs], writes=[Ds])
        P.act(Ds[:], Ds[:], AF.Exp, reads=[Ds], writes=[Ds], scale=-1.0)
        P.ts('vector', DTs[:], DTs[:], 0.0, None, ALU.min, reads=[DTs], writes=[DTs])
        P.act(DTs[:], DTs[:], AF.Exp, reads=[DTs], writes=[DTs])
        P.tt('vector', Ds[:].rearrange("p (g x) -> p g x", x=512), Ds[:].rearrange("p (g x) -> p g x", x=512),
             LGT8[:].unsqueeze(1).to_broadcast([CH, 8, 512]), ALU.mult, reads=[Ds, LGT8], writes=[Ds])
        P.tt('vector', DTs[:].rearrange("p (g x) -> p g x", x=512), DTs[:].rearrange("p (g x) -> p g x", x=512),
             UGT8[:].unsqueeze(1).to_broadcast([CH, 8, 512]), ALU.mult, reads=[DTs, UGT8], writes=[DTs])

        olds = [F[0], F[1], F[2], F[3]]

        def al(ap, name):
            return P.alias_after(ap, olds, name)
        u_t = al(farena[0:CH, 0:2 * S].rearrange("p (c e) -> p c e", e=128), "u_t")
        o = 2 * S
        wsM = [al(farena[0:CH, o + i * 512:o + (i + 1) * 512], f"wsM{i}") for i in range(2)]
        o += 1024
        wsT = [al(farena[0:CH, o + i * 512:o + (i + 1) * 512], f"wsT{i}") for i in range(2)]
        o += 1024
        wsP = [al(farena[0:CH, o + i * 512:o + (i + 1) * 512], f"wsP{i}") for i in range(2)]
        o += 1024
        TTb = al(farena[0:CH, o:o + S // 2], "TTb")
        TTb_ap = farena[0:CH, o:o + S // 2].bitcast(BF16)
        o += S // 2
        GW = GS * 128
        og = [al(farena[0:CH, o + i * GW:o + (i + 1) * GW], f"og{i}") for i in range(2)]
        o += 2 * GW
        zg = [al(farena[0:CH, o + i * GW:o + (i + 1) * GW], f"zg{i}") for i in range(2)]
        o += 2 * GW
        tg = al(farena[0:CH, o:o + GW], "tg")
        o += GW
        dtmp = al(farena[0:CH, o:o + 512], "dtmp")
        o += 512
        ss = ss_t
        assert o <= 4 * S

        for g8 in range(NCH // 8):
            gsl = slice(g8 * 512, (g8 + 1) * 512)
            pN, pA_ = banks[0], banks[1]
            for i in range(8):
                c = g8 * 8 + i
                cs = slice(c * CH, (c + 1) * CH)
                P.mm(pN[0:CH, i * CH:(i + 1) * CH], k_b[:, cs], kb_b[:, cs], reads=[k_b, kb_b], writes=[pN])
                P.mm(pA_[0:CH, i * CH:(i + 1) * CH], kb_b[:, cs], k_b[:, cs], reads=[k_b, kb_b], writes=[pA_])
            M, MT, Pk = wsM[0], wsT[0], wsP[0]
            P.tt('vector', M[:], pN[0:CH, :], DTs[:, gsl], ALU.mult, reads=[pN, DTs], writes=[M])
            P.tt('vector', MT[:], pA_[0:CH, :], Ds[:, gsl], ALU.mult, reads=[pA_, Ds], writes=[MT])
            P.tt('gpsimd', Pk[:], I8[:], M[:], ALU.subtract, reads=[I8, M], writes=[Pk])
            for k in range(1, 6):
                M2, MT2, Pk2 = wsM[k % 2], wsT[k % 2], wsP[k % 2]
                pM, pMT, pP = banks[2], banks[3], banks[4]
                for i in range(8):
                    bs = slice(i * CH, (i + 1) * CH)
                    if k < 5:
                        P.mm(pM[0:CH, bs], MT[:, bs], M[:, bs], reads=[MT, M], writes=[pM])
                    P.mm(pMT[0:CH, bs], M[:, bs], MT[:, bs], reads=[MT, M], writes=[pMT])
                if k < 5:
                    P.copy('scalar', M2[:], pM[0:CH, :], reads=[pM], writes=[M2])
                P.copy('vector', MT2[:], pMT[0:CH, :], reads=[pMT], writes=[MT2])
                for i in range(8):
                    bs = slice(i * CH, (i + 1) * CH)
                    P.mm(pP[0:CH, bs], MT2[:, bs], Pk[:, bs], reads=[MT2, Pk], writes=[pP])
                P.tt('vector', Pk2[:], Pk[:], pP[0:CH, :], ALU.add, reads=[Pk, pP], writes=[Pk2])
                M, MT, Pk = M2, MT2, Pk2
            P.copy('scalar', TTb_ap[:, gsl], Pk[:], reads=[Pk], writes=[TTb])
            pb2 = banks[7]
            for i in range(8):
                c = g8 * 8 + i
                cs = slice(c * CH, (c + 1) * CH)
                P.mm(pb2[0:CH, i * CH:(i + 1) * CH], k_b[:, cs], q_b[:, cs], reads=[k_b, q_b], writes=[pb2])
            P.tt('gpsimd', dtmp[:], DTs[:, gsl], I8[:], ALU.add, reads=[DTs, I8], writes=[dtmp])
            P.tt('vector', attT[:, gsl], pb2[0:CH, :], dtmp[:], ALU.mult, reads=[pb2, dtmp], writes=[attT])
        wT = kb_b
        for c4 in range(NCH // 4):
            pb = banks[5]
            for i in range(4):
                c = c4 * 4 + i
                P.mm(pb[0:CH, i * 128:(i + 1) * 128], TTb_ap[:, c * CH:(c + 1) * CH], vb_t[:, c, :],
                     reads=[TTb, vb_t], writes=[pb])
            P.copy('vector', u_t[:, c4 * 4:(c4 + 1) * 4, :], pb[0:CH, :].rearrange("p (a d) -> p a d", d=128),
                   reads=[pb], writes=[u_t])
        for g8 in range(NCH // 8):
            pb = banks[6]
            for i in range(8):
                c = g8 * 8 + i
                cs = slice(c * CH, (c + 1) * CH)
                P.mm(pb[:, i * CH:(i + 1) * CH], kbe_t[:, c, :], TTb_ap[:, cs], reads=[kbe_t, TTb], writes=[pb])
            P.copy('scalar', wT[:, g8 * 512:(g8 + 1) * 512], pb[:], reads=[pb], writes=[wT])
        P.memset('vector', Sst[:], 0.0, writes=[Sst])
        P.memset('gpsimd', Sbf[:], 0.0, writes=[Sbf])
        yv = d["out"].rearrange("(c j) e -> j c e", j=CH)
        zv = d["gz"].rearrange("(c j) e -> j c e", j=CH)
        for c in range(NCH):
            cs = slice(c * CH, (c + 1) * CH)
            gi, ii = divmod(c, GS)
            og_ = og[gi % 2]
            zg_ = zg[gi % 2]
            if ii == 0:
                P.dma('sync', zg_[:].rearrange("p (a e) -> p a e", e=128), zv[:, gi * GS:(gi + 1) * GS, :], writes=[zg_])
            pW = banks[c % 2]
            pO = banks[2 + c % 2]
            pS = banks[4 + c % 2]
            vn_ = vn[c % 2]
            P.mm(pW[0:CH, 0:128], wT[:, cs], Sbf[:], reads=[wT, Sbf], writes=[pW])
            P.tt('vector', vn_[:], u_t[:, c, :], pW[0:CH, 0:128], ALU.subtract, reads=[u_t, pW], writes=[vn_])
            P.mm(pO[0:CH, 0:128], qd_b[:, cs], Sbf[:], start=True, stop=False, reads=[qd_b, Sbf], writes=[pO])
            P.mm(pO[0:CH, 0:128], attT[:, cs], vn_[:], start=False, stop=True, reads=[attT, vn_], writes=[pO])
            P.copy('scalar', og_[:, ii * 128:(ii + 1) * 128], pO[0:CH, 0:128], reads=[pO], writes=[og_])
            P.mm(pS[:, 0:128], kdec_t[:, c, :], vn_[:], reads=[kdec_t, vn_], writes=[pS])
            P.stt('vector', Sst[:], Sst[:], egl[:, c:c + 1], pS[:, 0:128], ALU.mult, ALU.add,
                  reads=[Sst, egl, pS], writes=[Sst])
            P.copy('scalar', Sbf[:], Sst[:], reads=[Sst], writes=[Sbf])
            if ii == GS - 1:
                o3 = og_[:].rearrange("p (a e) -> p a e", e=128)
                t3 = tg[:].rearrange("p (a e) -> p a e", e=128)
                P.tt('gpsimd', tg[:], og_[:], og_[:], ALU.mult, reads=[og_], writes=[tg])
                P.op('vector', lambda e, t3=t3: e.tensor_reduce(ss[:, 0:GS], t3, axis=AX.X, op=ALU.add), [tg], [ss])
                P.act(ss[:, 8:8 + GS], ss[:, 0:GS], AF.Sqrt, reads=[ss], writes=[ss], bias=EPSB[0][0:CH, 0:1], scale=1.0 / 128)
                P.op('vector', lambda e: e.reciprocal(ss[:, 8:8 + GS], ss[:, 8:8 + GS]), [ss], [ss])
                P.tt('vector', o3, o3, ss[:, 8:8 + GS].unsqueeze(2).to_broadcast([CH, GS, 128]), ALU.mult,
                     reads=[og_, ss], writes=[og_])
                P.tt('gpsimd', og_[:], og_[:], nwb[:, 0:GW], ALU.mult, reads=[og_, nwb], writes=[og_])
                P.act(zg_[:], zg_[:], AF.Silu, reads=[zg_], writes=[zg_])
                P.tt('vector', og_[:], og_[:], zg_[:], ALU.mult, reads=[og_, zg_], writes=[og_])
                P.dma('sync', yv[:, gi * GS:(gi + 1) * GS, :], o3, reads=[og_], writes=[outb], is_output=True)
        if s + 1 < NSLOT:
            al_all = [u_t] + wsM + wsT + wsP + [TTb] + og + zg + [tg, ss, dtmp, Ds, DTs]
            for i in range(6):
                F[i] = P.alias_after(farena[:, i * S:(i + 1) * S], al_all, f"F{i}_s{s + 1}")
    P.finish()
    return nc


def run_gdn(pT, layer, gdn_conv, gdn_a_log, gdn_dt_bias, gdn_norm):
    nc = _get("gdn", build_gdn)
    o_q, o_k, o_v, o_a, o_b, o_z = (IN_OFFS[i] for i in (8, 9, 10, 11, 12, 13))
    units = [(b, h) for b in range(B) for h in range(GDN_H)]
    in_maps, slots = [], []
    nwb = np.ascontiguousarray(np.broadcast_to(np.tile(gdn_norm[layer], 8)[None, :], (CH, 8 * 128))).astype(np.float32)
    for c in range(NCORES):
        m, us = {}, []
        for s in range(NSLOT):
            ui = c + s * NCORES
            real = ui < len(units)
            if not real:
                ui = c
            b, h = units[ui]
            us.append((b, h, real))
            ts = slice(b * S, (b + 1) * S)
            rs = slice(h * 128, (h + 1) * 128)
            m[f"gq{s}"] = np.ascontiguousarray(pT[o_q + h * 128:o_q + (h + 1) * 128, ts])
            m[f"gk{s}"] = np.ascontiguousarray(pT[o_k + h * 128:o_k + (h + 1) * 128, ts])
            m[f"gv{s}"] = np.ascontiguousarray(pT[o_v + h * 128:o_v + (h + 1) * 128, ts])
            m[f"gz{s}"] = np.ascontiguousarray(pT[o_z + h * 128:o_z + (h + 1) * 128, ts].T)
            m[f"gab{s}"] = np.ascontiguousarray(np.broadcast_to(pT[o_a + h:o_a + h + 1, ts], (128, S)))
            m[f"gbb{s}"] = np.ascontiguousarray(np.broadcast_to(pT[o_b + h:o_b + h + 1, ts], (128, S)))
            par = np.zeros((128, 16), np.float32)
            for mi in range(3):
                par[:, mi * 4:(mi + 1) * 4] = gdn_conv[layer][:, mi * 640 + h * 128:mi * 640 + (h + 1) * 128].T
            par[:, 12] = gdn_a_log[layer, h]
            par[:, 13] = gdn_dt_bias[layer, h]
            m[f"gpar{s}"] = par
            m[f"gnw{s}"] = nwb
        in_maps.append(m)
        slots.append(us)
    res = _launch(nc, in_maps)
    y = np.zeros((GDN_H * 128, NTOK), np.float32)
    for c in range(NCORES):
        for s in range(NSLOT):
            b, h, real = slots[c][s]
            if real:
                y[h * 128:(h + 1) * 128, b * S:(b + 1) * S] = res[c][f"gy{s}"].T
    return y


NQI = 16
NEG = -30000.0


def build_nsa():
    nc = bass.Bass("TRN2", target_bir_lowering=False)

    def din(name, shape):
        return nc.dram_tensor(name, list(shape), F32, kind="ExternalInput").ap()
    qin = din("qin", [3, 128, NQI * 128])
    kc_d, ks_d, kw_d, vc_d = (din(n, [128, S]) for n in ("kc", "ks", "kw", "vc"))
    vs_d, vw_d = din("vs", [S, 128]), din("vw", [S, 128])
    gl_d = din("gl", [128, NQI, 9])
    cs_d, sn_d = din("cs", [128, S]), din("sn", [128, S])
    csq_d, snq_d = din("csq", [128, NQI * 128]), din("snq", [128, NQI * 128])
    psw_d = din("psw", [128, 128])
    pek_d, pev_d = din("pek", [128, 32]), din("pev", [128, 32])
    w1k_d, w1v_d = din("w1k", [4096, 128]), din("w1v", [4096, 128])
    w2k_d, w2v_d = din("w2k", [128, 128]), din("w2v", [128, 128])
    ovl_d = din("ovl", [128, 2, 64])
    emat_d = din("emat", [64, S])
    masks_d = din("masks", [128, 4, 384])
    cbase_d = din("cbase", [128, 128])
    coffs_d = din("coffs", [128, 32])
    keep_d, addv_d = din("keep", [128, NQI, 64]), din("addv", [128, NQI, 64])
    y_d = nc.dram_tensor("y", [128, NQI, 384], F32, kind="ExternalOutput").ap()

    P = Prog(nc)
    ones = make_consts(P)
    ident = make_ident(P)
    identb = P.sb([128, 128], BF16, "identb")
    P.copy('vector', identb[:], ident[:], reads=[ident], writes=[identb])
    psw = P.sb([128, 128], F32, "psw")
    P.dma('sync', psw[:], psw_d, writes=[psw])
    CS = P.sb([128, S], F32, "CS")
    SN = P.sb([128, S], F32, "SN")
    X = P.sb([128, S], F32, "X")
    T1 = P.sb([128, S], F32, "T1")
    T2 = P.sb([128, S], F32, "T2")
    ksT = P.sb([128, S], BF16, "ksT")
    kwT = P.sb([128, S], BF16, "kwT")
    kcT = P.sb([128, S], BF16, "kcT")
    vcT = P.sb([128, S], BF16, "vcT")
    qsel = P.sb([128, NQI, 384], BF16, "qsel")
    vs_aug = P.sb([128, 32, 129], BF16, "vs_aug")
    vw_aug = P.sb([128, 32, 129], BF16, "vw_aug")
    emat = P.sb([64, S], BF16, "emat")
    masks = P.sb([128, 4, 384], BF16, "masks")
    w1k = P.sb([128, 32, 128], BF16, "w1k")
    w1v = P.sb([128, 32, 128], BF16, "w1v")
    w2k = P.sb([128, 128], BF16, "w2k")
    w2v = P.sb([128, 128], BF16, "w2v")
    pek = P.sb([128, 32], BF16, "pek")
    pev = P.sb([128, 32], BF16, "pev")
    cbase = P.sb([128, 128], F32, "cbase")
    coffs = P.sb([128, 32], F32, "coffs")
    keep = P.sb([128, NQI, 64], F32, "keep")
    addv = P.sb([128, NQI, 64], F32, "addv")
    gates = P.sb([128, NQI, 9], F32, "gates")
    kcmpT = P.sb([128, 256], BF16, "kcmpT")
    vcmp = P.sb([128, 2, 193], BF16, "vcmp")
    h1 = P.sb([128, 256], BF16, "h1")
    cb = P.sb([128, 2], F32, "cb")
    banks = [P.ps([128, 512], F32, f"bk{i}") for i in range(8)]

    P.dma('sync', CS[:], cs_d, writes=[CS])
    P.dma('sync', SN[:], sn_d, writes=[SN])
    P.dma('gpsimd', emat[:], emat_d, writes=[emat])
    P.dma('gpsimd', masks[:], masks_d, writes=[masks])
    P.dma('gpsimd', w1k[:], w1k_d.rearrange("(l d) h -> d l h", d=128), writes=[w1k])
    P.dma('gpsimd', w1v[:], w1v_d.rearrange("(l d) h -> d l h", d=128), writes=[w1v])
    P.dma('gpsimd', w2k[:], w2k_d, writes=[w2k])
    P.dma('gpsimd', w2v[:], w2v_d, writes=[w2v])
    P.dma('gpsimd', pek[:], pek_d, writes=[pek])
    P.dma('gpsimd', pev[:], pev_d, writes=[pev])
    P.dma('gpsimd', vcT[:], vc_d, writes=[vcT])
    P.dma('sync', cbase[:], cbase_d, writes=[cbase])
    P.dma('sync', coffs[:], coffs_d, writes=[coffs])
    P.dma('sync', keep[:], keep_d, writes=[keep])
    P.dma('sync', addv[:], addv_d, writes=[addv])
    P.dma('sync', gates[:], gl_d, writes=[gates])
    P.act(gates[:], gates[:], AF.Sigmoid, reads=[gates], writes=[gates])
    for (aug, src) in ((vs_aug, vs_d), (vw_aug, vw_d)):
        P.memset('vector', aug[:, :, 128:129], 1.0, writes=[aug])
        P.dma('gpsimd', aug[:, :, 0:128], src.rearrange("(kt p) e -> p kt e", p=128), writes=[aug])

    def rope(x_dram, n, cst, snt, dst3, scale, dstbuf):
        P.dma('sync', X[:, 0:n], x_dram, writes=[X])
        for h in range(n // 512):
            pb = banks[h % 4]
            sl = slice(h * 512, (h + 1) * 512)
            P.mm(pb[:], psw[:], X[:, sl], reads=[psw, X], writes=[pb])
            P.tt('vector', T2[:, sl], pb[:], snt[:, sl], ALU.mult, reads=[pb, snt], writes=[T2])
        P.tt('gpsimd', T1[:, 0:n], X[:, 0:n], cst[:, 0:n], ALU.mult, reads=[X, cst], writes=[T1])
        P.tt('vector', T1[:, 0:n], T1[:, 0:n], T2[:, 0:n], ALU.add, reads=[T1, T2], writes=[T1])
        src = T1[:, 0:n]
        if dst3 is not None:
            src = src.rearrange("p (a q) -> p a q", q=128)
            P.op('scalar', lambda e: e.mul(dst3, src, scale), [T1], [dstbuf])
        else:
            P.op('scalar', lambda e: e.mul(dstbuf[:], src, scale), [T1], [dstbuf])

    rope(kc_d, S, CS, SN, None, 1.0, kcT)
    rope(ks_d, S, CS, SN, None, 1.0, ksT)
    rope(kw_d, S, CS, SN, None, 1.0, kwT)
    P.dma('sync', CS[:, 0:NQI * 128], csq_d, writes=[CS])
    P.dma('sync', SN[:, 0:NQI * 128], snq_d, writes=[SN])
    for r in range(3):
        rope(qin[r], NQI * 128, CS, SN, qsel[:, :, r * 128:(r + 1) * 128], float(HD) ** -0.5, qsel)

    P.memset('vector', vcmp[:], 0.0, writes=[vcmp])
    P.memset('vector', h1[:], 0.0, writes=[h1])
    P.memset('vector', kcmpT[:], 0.0, writes=[kcmpT])
    NCMP = 255
    for which, (srcT, w1, pe, w2) in enumerate(((kcT, w1k, pek, w2k), (vcT, w1v, pev, w2v))):
        pb, pbias = banks[4], banks[5]
        for l in range(32):
            P.mm(pb[:, 0:NCMP], w1[:, l, :], srcT[:, l:l + 16 * (NCMP - 1) + 1:16], start=(l == 0), stop=(l == 31),
                 reads=[w1, srcT], writes=[pb])
        for l in range(32):
            P.mm(pbias[:, 0:1], w1[:, l, :], pe[:, l:l + 1], start=(l == 0), stop=(l == 31),
                 reads=[w1, pe], writes=[pbias])
        P.copy('vector', cb[:, which:which + 1], pbias[:, 0:1], reads=[pbias], writes=[cb])
        P.act(h1[:, 0:NCMP], pb[:, 0:NCMP], AF.Gelu_apprx_tanh, reads=[pb, cb], writes=[h1], bias=cb[:, which:which + 1])
        if which == 0:
            pk = banks[6]
            P.mm(pk[:, 0:NCMP], w2[:], h1[:, 0:NCMP], reads=[w2, h1], writes=[pk])
            P.copy('vector', kcmpT[:, 0:NCMP], pk[:, 0:NCMP], reads=[pk], writes=[kcmpT])
        else:
            pv = banks[7]
            P.mm(pv[:, 0:128], h1[:, 0:128], w2[:], reads=[w2, h1], writes=[pv])
            P.mm(pv[0:127, 128:256], h1[:, 128:255], w2[:], reads=[w2, h1], writes=[pv])
            P.copy('vector', vcmp[:, 0, 0:128], pv[:, 0:128], reads=[pv], writes=[vcmp])
            P.copy('vector', vcmp[0:127, 1, 0:128], pv[0:127, 128:256], reads=[pv], writes=[vcmp])
    P.memset('vector', vcmp[:, 0, 128:129], 1.0, writes=[vcmp])
    P.memset('vector', vcmp[0:127, 1, 128:129], 1.0, writes=[vcmp])
    ovl = P.sb([128, 2, 64], F32, "ovl")
    P.dma('sync', ovl[:], ovl_d, writes=[ovl])
    P.copy('vector', vcmp[:, :, 129:193], ovl[:], reads=[ovl], writes=[vcmp])

    PT = [P.sb([128, 384], BF16, f"PT{i}") for i in range(3)]
    m01 = [P.sb([128, 128], BF16, f"m01_{i}") for i in range(2)]
    OUT = [P.sb([128, 384], F32, f"OUT{i}") for i in range(2)]
    imp = P.sb([128, 64], F32, "imp")
    imp2 = P.sb([128, 64], F32, "imp2")
    wrk = P.sb([128, 64], F32, "wrk")
    m8 = P.sb([128, 16], F32, "m8")
    seln = P.sb([128, 64], F32, "seln")
    negT3 = P.sb([64, 384], BF16, "negT3")
    sm = P.sb([128, 16], F32, "sm")
    outb = P.view(None, "y_out")
    npt = [0]
    nacc = [0]

    def evac(acc, r, qi, gate_j, out_, first):
        c0 = 2 * (r * 3 + gate_j) % 16
        P.ts('vector', sm[:, c0:c0 + 1], acc[:, 128:129], 1e-30, None, ALU.max, reads=[acc], writes=[sm])
        P.op('vector', lambda e: e.reciprocal(sm[:, c0:c0 + 1], sm[:, c0:c0 + 1]), [sm], [sm])
        P.tt('vector', sm[:, c0 + 1:c0 + 2], sm[:, c0:c0 + 1], gates[:, qi, r * 3 + gate_j:r * 3 + gate_j + 1], ALU.mult,
             reads=[sm, gates], writes=[sm])
        osl = out_[:, r * 128:(r + 1) * 128]
        if first:
            P.ts('vector', osl, acc[:, 0:128], sm[:, c0 + 1:c0 + 2], None, ALU.mult, reads=[acc, sm], writes=[out_])
        else:
            P.stt('vector', osl, acc[:, 0:128], sm[:, c0 + 1:c0 + 2], osl, ALU.mult, ALU.add,
                  reads=[acc, sm, out_], writes=[out_])
        return c0

    for qi in range(NQI):
        out_ = OUT[qi % 2]
        qr = qsel[:, qi, :]
        accs = [banks[2 + (nacc[0] % 2) * 3 + r] for r in range(3)]
        nacc[0] += 1
        nts = 2 if qi >= 8 else 1
        for nt in range(nts):
            sc = banks[npt[0] % 2]
            pt = PT[npt[0] % 3]
            mm_ = m01[npt[0] % 2]
            npt[0] += 1
            P.mm(sc[:, 0:384], kcmpT[:, nt * 128:(nt + 1) * 128], qr, reads=[kcmpT, qsel], writes=[sc])
            P.act(pt[:], sc[:, 0:384], AF.Exp, reads=[sc], writes=[pt])
            P.ts('gpsimd', mm_[:], cbase[:], coffs[:, qi * 2 + nt:qi * 2 + nt + 1], 0.0, ALU.add, ALU.is_ge,
                 reads=[cbase, coffs], writes=[mm_])
            p3 = pt[:].rearrange("p (r q) -> p r q", q=128)
            P.tt('vector', p3, p3, mm_[:].unsqueeze(1).to_broadcast([128, 3, 128]), ALU.mult,
                 reads=[pt, mm_], writes=[pt])
            for r in range(3):
                P.mm(accs[r][:, 0:193], pt[:, r * 128:(r + 1) * 128], vcmp[:, nt, :], start=(nt == 0),
                     stop=(nt == nts - 1), reads=[pt, vcmp], writes=[accs[r]])
        for r in range(3):
            c0 = evac(accs[r], r, qi, 0, out_, True)
            if r == 0:
                P.ts('vector', imp[:], accs[r][:, 129:193], sm[:, c0:c0 + 1], None, ALU.mult,
                     reads=[accs[r], sm], writes=[imp])
            else:
                P.stt('vector', imp[:], accs[r][:, 129:193], sm[:, c0:c0 + 1], imp[:], ALU.mult, ALU.add,
                      reads=[accs[r], sm, imp], writes=[imp])
        P.tt('vector', imp2[:], imp[:], keep[:, qi, :], ALU.mult, reads=[imp, keep], writes=[imp2])
        P.tt('vector', imp2[:], imp2[:], addv[:, qi, :], ALU.add, reads=[imp2, addv], writes=[imp2])
        P.op('vector', lambda e: e.max(out=m8[:, 0:8], in_=imp2[:]), [imp2], [m8])
        P.op('vector', lambda e: e.match_replace(out=wrk[:], in_to_replace=m8[:, 0:8], in_values=imp2[:],
                                                 imm_value=-3.0e38), [imp2, m8], [wrk])
        P.op('vector', lambda e: e.max(out=m8[:, 8:16], in_=wrk[:]), [wrk], [m8])
        P.ts('vector', seln[:], imp2[:], m8[:, 15:16], 0.0, ALU.subtract, ALU.is_ge, reads=[imp2, m8], writes=[seln])
        P.ts('vector', seln[:], seln[:], -NEG, NEG, ALU.mult, ALU.add, reads=[seln], writes=[seln])
        ptr = banks[npt[0] % 2]
        P.op('tensor', lambda e, ptr=ptr: e.transpose(ptr[0:64, 0:128], seln[:], ident[:]), [seln, ident], [ptr])
        P.copy('vector', negT3[:].rearrange("p (r q) -> p r q", q=128),
               ptr[0:64, 0:128].unsqueeze(1).to_broadcast([64, 3, 128]), reads=[ptr], writes=[negT3])
        accs = [banks[2 + (nacc[0] % 2) * 3 + r] for r in range(3)]
        nacc[0] += 1
        nkt = 2 * qi + 2
        for kt in range(nkt):
            sc = banks[npt[0] % 2]
            pt = PT[npt[0] % 3]
            npt[0] += 1
            ks_ = slice(kt * 128, (kt + 1) * 128)
            dm = kt >= 2 * qi
            P.mm(sc[:, 0:384], ksT[:, ks_], qr, start=True, stop=False, reads=[ksT, qsel], writes=[sc])
            P.mm(sc[:, 0:384], emat[:, ks_], negT3[:], start=False, stop=not dm, reads=[emat, negT3], writes=[sc])
            if dm:
                P.mm(sc[:, 0:384], identb[:], masks[:, kt - 2 * qi, :], start=False, stop=True,
                     reads=[identb, masks], writes=[sc])
            P.act(pt[:], sc[:, 0:384], AF.Exp, reads=[sc], writes=[pt])
            for r in range(3):
                P.mm(accs[r][:, 0:129], pt[:, r * 128:(r + 1) * 128], vs_aug[:, kt, :], start=(kt == 0),
                     stop=(kt == nkt - 1), reads=[pt, vs_aug], writes=[accs[r]])
        for r in range(3):
            evac(accs[r], r, qi, 1, out_, False)
        accs = [banks[2 + (nacc[0] % 2) * 3 + r] for r in range(3)]
        nacc[0] += 1
        poss = [p_ for p_ in range(6) if 2 * qi - 4 + p_ >= 0]
        for p_ in poss:
            kt = 2 * qi - 4 + p_
            sc = banks[npt[0] % 2]
            pt = PT[npt[0] % 3]
            npt[0] += 1
            ks_ = slice(kt * 128, (kt + 1) * 128)
            mi = {0: 2, 1: 3, 4: 0, 5: 1}.get(p_)
            P.mm(sc[:, 0:384], kwT[:, ks_], qr, start=True, stop=(mi is None), reads=[kwT, qsel], writes=[sc])
            if mi is not None:
                P.mm(sc[:, 0:384], identb[:], masks[:, mi, :], start=False, stop=True,
                     reads=[identb, masks], writes=[sc])
            P.act(pt[:], sc[:, 0:384], AF.Exp, reads=[sc], writes=[pt])
            for r in range(3):
                P.mm(accs[r][:, 0:129], pt[:, r * 128:(r + 1) * 128], vw_aug[:, kt, :], start=(p_ == poss[0]),
                     stop=(p_ == poss[-1]), reads=[pt, vw_aug], writes=[accs[r]])
        for r in range(3):
            evac(accs[r], r, qi, 2, out_, False)
        P.dma('sync', y_d[:, qi, :], out_[:], reads=[out_], writes=[outb], is_output=True)
    P.finish()
    return nc


def _nsa_consts():
    inv = 1.0 / (10000.0 ** (np.arange(0, HD, 2, dtype=np.float32) / HD))
    ang = np.arange(S, dtype=np.float32)[:, None] * inv[None, :]
    cos, sin = np.cos(ang).T.astype(np.float32), np.sin(ang).T.astype(np.float32)
    cs = np.concatenate([cos, cos], axis=0)
    sn = np.concatenate([-sin, sin], axis=0)
    psw = np.zeros((128, 128), np.float32)
    for m in range(128):
        psw[(m + 64) % 128, m] = 1.0
    n = np.arange(256)
    c_start = n * 16
    s_start = np.arange(64) * 64
    ov = np.minimum(c_start[:, None] + 32, s_start[None, :] + 64) - np.maximum(c_start[:, None], s_start[None, :])
    ovl = (np.clip(ov, 0, None).astype(np.float32) / 32.0)
    ovl[255] = 0.0
    ovl = np.ascontiguousarray(ovl.reshape(2, 128, 64).transpose(1, 0, 2))
    emat = np.zeros((64, S), np.float32)
    emat[np.arange(S) // 64, np.arange(S)] = 1.0
    j = np.arange(128)[:, None]
    i = np.arange(128)[None, :]
    causal = np.where(j <= i, 0.0, NEG).astype(np.float32)
    far = np.where(j > i, 0.0, NEG).astype(np.float32)
    zero = np.zeros((128, 128), np.float32)
    full = np.full((128, 128), NEG, np.float32)
    cbase = (i - 16 * j).astype(np.float32)
    per_par = []
    for par in range(2):
        ms = [causal, full, far, zero] if par == 0 else [zero, causal, full, far]
        masks = np.stack([np.tile(m, (1, 3)) for m in ms], axis=1).astype(np.float32)
        coffs = np.zeros((128, 32), np.float32)
        keep = np.zeros((128, NQI, 64), np.float32)
        addv = np.zeros((128, NQI, 64), np.float32)
        blk = np.arange(64)[None, :]
        for qi in range(NQI):
            qt = 2 * qi + par
            for nt in range(2):
                coffs[:, qi * 2 + nt] = 128 * qt - 2048 * nt - 31
            t = qt * 128 + np.arange(128)[:, None]
            cur = t // 64
            forced = (blk == 0) | (blk == cur) | (blk == cur - 1)
            future = (blk * 64 > t) & ~forced
            keep[:, qi, :] = (~forced & ~future)
            addv[:, qi, :] = 1e9 * forced - 1e9 * future
        sel_cols = np.concatenate([np.arange((2 * qi + par) * 128, (2 * qi + par + 1) * 128) for qi in range(NQI)])
        per_par.append(dict(masks=masks, coffs=coffs, keep=keep, addv=addv, sel=sel_cols,
                            csq=np.ascontiguousarray(cs[:, sel_cols]), snq=np.ascontiguousarray(sn[:, sel_cols])))
    return dict(cs=cs, sn=sn, psw=psw, ovl=ovl, emat=emat, cbase=cbase, per_par=per_par)


def run_nsa(pT, layer, pe_k, pe_v, ck1, ck2, cv1, cv2):
    nc = _get("nsa", build_nsa)
    C = _get("nsa_consts", _nsa_consts)
    o_q, o_kc, o_vc, o_ks, o_vs, o_kw, o_vw, o_gt = (IN_OFFS[i] for i in range(8))
    in_maps, units = [], []
    for c in range(NCORES):
        b, g, par = c // 4, (c // 2) % 2, c % 2
        units.append((b, g, par))
        pp = C["per_par"][par]
        ts = slice(b * S, (b + 1) * S)
        sel = pp["sel"] + b * S

        def fm(off):
            return np.ascontiguousarray(pT[off + g * 128:off + (g + 1) * 128, ts])
        m = dict(kc=fm(o_kc), ks=fm(o_ks), kw=fm(o_kw), vc=fm(o_vc),
                 vs=np.ascontiguousarray(fm(o_vs).T), vw=np.ascontiguousarray(fm(o_vw).T),
                 cs=C["cs"], sn=C["sn"], csq=pp["csq"], snq=pp["snq"], psw=C["psw"],
                 pek=np.ascontiguousarray(pe_k[layer].T), pev=np.ascontiguousarray(pe_v[layer].T),
                 w1k=np.ascontiguousarray(ck1[layer]), w1v=np.ascontiguousarray(cv1[layer]),
                 w2k=np.ascontiguousarray(ck2[layer]), w2v=np.ascontiguousarray(cv2[layer]),
                 ovl=C["ovl"], emat=C["emat"], masks=pp["masks"], cbase=C["cbase"], coffs=pp["coffs"],
                 keep=pp["keep"], addv=pp["addv"])
        m["qin"] = np.stack([pT[o_q + (g * 3 + r) * 128:o_q + (g * 3 + r + 1) * 128, sel] for r in range(3)]).astype(np.float32)
        glt = pT[o_gt + g * 9:o_gt + (g + 1) * 9, sel]
        m["gl"] = np.ascontiguousarray(glt.reshape(9, NQI, 128).transpose(2, 1, 0))
        in_maps.append(m)
    res = _launch(nc, in_maps)
    y = np.zeros((NSA_H * 128, NTOK), np.float32)
    for c in range(NCORES):
        b, g, par = units[c]
        sel = C["per_par"][par]["sel"] + b * S
        yo = res[c]["y"]
        for r in range(3):
            blk = yo[:, :, r * 128:(r + 1) * 128]
            y[(g * 3 + r) * 128:(g * 3 + r + 1) * 128, sel] = blk.transpose(2, 1, 0).reshape(128, NQI * 128)
    return y


def build_fnorm():
    T = TPC
    nc = bass.Bass("TRN2", target_bir_lowering=False)
    xT = nc.dram_tensor("xT", [D, T], F32, kind="ExternalInput").ap()
    g = nc.dram_tensor("g", [128, NDC], F32, kind="ExternalInput").ap()
    oT = nc.dram_tensor("oT", [D, T], F32, kind="ExternalOutput").ap()
    P = Prog(nc)
    ones = make_consts(P)
    g_sb = P.sb([128, NDC], F32, "g")
    P.dma('sync', g_sb[:], g, writes=[g_sb])
    xs = [P.sb([128, T], F32, f"xs{i}") for i in range(2)]
    os_ = [P.sb([128, T], F32, f"os{i}") for i in range(2)]
    sq = P.sb([128, T], F32, "sq")
    rstd = P.sb([128, T], F32, "rstd")
    banks = [P.ps([128, 512], F32, f"bk{i}") for i in range(2)]
    nh = T // 512
    for dc in range(NDC):
        x_ = xs[dc % 2]
        P.dma('sync', x_[:], xT[dc * 128:(dc + 1) * 128, :], writes=[x_])
        P.act(sq[:], x_[:], AF.Square, reads=[x_], writes=[sq])
        for h in range(nh):
            P.mm(banks[h][:], ones[:], sq[:, h * 512:(h + 1) * 512], start=(dc == 0), stop=(dc == NDC - 1),
                 reads=[ones, sq], writes=[banks[h]])
    for h in range(nh):
        P.act(rstd[:, h * 512:(h + 1) * 512], banks[h][:], AF.Sqrt, reads=[banks[h]], writes=[rstd],
              bias=EPSB[0][:, 0:1], scale=1.0 / D)
    P.op('vector', lambda e: e.reciprocal(rstd[:], rstd[:]), [rstd], [rstd])
    outb = P.view(None, "oT_out")
    for dc in range(NDC):
        x_ = xs[dc % 2]
        o_ = os_[dc % 2]
        P.dma('sync', x_[:], xT[dc * 128:(dc + 1) * 128, :], writes=[x_])
        P.stt('vector', o_[:], x_[:], g_sb[:, dc:dc + 1], rstd[:], ALU.mult, ALU.mult, reads=[x_, g_sb, rstd], writes=[o_])
        P.dma('sync', oT[dc * 128:(dc + 1) * 128, :], o_[:], reads=[o_], writes=[outb], is_output=True)
    P.finish()
    return nc


def run_fnorm(xT_full, gnorm):
    nc = _get("fnorm", build_fnorm)
    gl = gain_layout(gnorm)
    in_maps = [{"xT": np.ascontiguousarray(xT_full[:, c * TPC:(c + 1) * TPC]), "g": gl} for c in range(NCORES)]
    res = _launch(nc, in_maps)
    return np.concatenate([r["oT"] for r in res], axis=1)


def kernel(x, ffn1_norm, ffn1_gate, ffn1_up, ffn1_down, mix_norm, w_in, w_out,
           nsa_pe_k, nsa_pe_v, nsa_ck1, nsa_ck2, nsa_cv1, nsa_cv2,
           gdn_conv, gdn_a_log, gdn_dt_bias, gdn_norm, hgrn_lb, hgrn_norm,
           ffn2_norm, ffn2_gate, ffn2_up, ffn2_down, final_norm):
    A = lambda a: np.asarray(a, dtype=np.float32)
    xT = np.ascontiguousarray(A(x).reshape(NTOK, D).T)
    depth = A(ffn1_norm).shape[0]
    for l in range(depth):
        xT = run_ffn(xT, A(ffn1_norm)[l], A(ffn1_gate)[l], A(ffn1_up)[l], A(ffn1_down)[l])
        pT = run_inproj(xT, A(mix_norm)[l], A(w_in)[l])
        y_nsa = run_nsa(pT, l, A(nsa_pe_k), A(nsa_pe_v), A(nsa_ck1), A(nsa_ck2), A(nsa_cv1), A(nsa_cv2))
        y_gdn = run_gdn(pT, l, A(gdn_conv), A(gdn_a_log), A(gdn_dt_bias), A(gdn_norm))
        y_hg = run_hgrn(pT, l, A(hgrn_lb), A(hgrn_norm))
        yT = np.concatenate([y_nsa, y_gdn, y_hg], axis=0)
        xT = run_outproj(xT, yT, A(w_out)[l])
        xT = run_ffn(xT, A(ffn2_norm)[l], A(ffn2_gate)[l], A(ffn2_up)[l], A(ffn2_down)[l])
    oT = run_fnorm(xT, A(final_norm))
    return np.ascontiguousarray(oT.T).reshape(B, S, D).astype(np.float32)
```
